# Optimizing a Trainium2 kernel written in Bass

```python
import functools
import jax, jax.numpy as jnp
from jax import lax
import numpy as np

D_MODEL = 1024
BATCH = 4
SEQ = 4096
DEPTH = 4
DEC_BATCH = 32
DEC_SEQ = 4
PAST_LEN = 8192
PAGE_SIZE = 128

N_A_LAYERS = DEPTH // 2
N_B_LAYERS = DEPTH - N_A_LAYERS
A_KDIM = 128
A_HEADS = D_MODEL // A_KDIM
A_VDIM = D_MODEL // A_HEADS
A_CHUNK = 64
A_SUB = 16
HEAD_DIM = 128
N_GROUPS = 3
Q_HEADS = D_MODEL // HEAD_DIM
KV_HEADS = 2
WINDOWS = (128, 512, 2048)
DILATIONS = (1, 4, 16)
DIL_KEYS = 128
ROT_DIM = HEAD_DIM // 4
ROPE_THETA = 500000.0
D_FF = 4 * D_MODEL
EPS = 1e-6
NEG = -1e30
EXP_CLAMP = 80.0

kernel_name = 'yoco_hgrn2_dilated_swa_decode_step'


def _rmsnorm(x, gain):
    xf = x.astype(jnp.float32)
    y = xf * lax.rsqrt(jnp.mean(xf * xf, axis=-1, keepdims=True) + EPS)
    return (y * gain.astype(jnp.float32)).astype(x.dtype)


def _head_rmsnorm(x, gain):
    return x * lax.rsqrt(jnp.mean(x * x, axis=-1, keepdims=True) + EPS) * gain.astype(jnp.float32)


def _rotary(x, pos):
    half = ROT_DIM // 2
    inv = ROPE_THETA ** (-2.0 * jnp.arange(half, dtype=jnp.float32) / ROT_DIM)
    ang = pos.astype(jnp.float32)[:, None] * inv[None, :]
    shape = (ang.shape[0],) + (1,) * (x.ndim - 3) + (half,)
    cos, sin = jnp.cos(ang).reshape(shape), jnp.sin(ang).reshape(shape)
    x1, x2 = x[..., :half], x[..., half:ROT_DIM]
    return jnp.concatenate([x1 * cos - x2 * sin, x2 * cos + x1 * sin, x[..., ROT_DIM:]], axis=-1)


def _sqrelu_mlp(x, w_up, w_down):
    return jnp.square(jax.nn.relu(x @ w_up)) @ w_down


def _masked_exp(mask, val):
    return jnp.where(mask, jnp.exp(jnp.where(mask, val, 0.0)), 0.0)


def _gla_chunk(state, inp, sub):
    q, k, v, g = inp
    B, C, H, K = q.shape
    ns = C // sub
    b = jnp.cumsum(g, axis=1)
    o_state = jnp.einsum('bchk,bhkv->bchv', q * jnp.exp(b), state)
    qs = q.reshape(B, ns, sub, H, K)
    ks = k.reshape(B, ns, sub, H, K)
    bs = b.reshape(B, ns, sub, H, K)
    vs = v.reshape(B, ns, sub, H, -1)
    ref = jnp.concatenate([jnp.zeros_like(bs[:, :1, 0]), bs[:, :-1, -1]], axis=1)
    qf = qs * jnp.exp(bs - ref[:, :, None])
    earlier = jnp.arange(C)[None, :] < (jnp.arange(ns) * sub)[:, None]
    kf = k[:, None] * _masked_exp(earlier[None, :, :, None, None], ref[:, :, None] - b[:, None])
    a_prev = jnp.einsum('bithk,bishk->bitsh', qf, kf)
    o_prev = jnp.einsum('bitsh,bshv->bithv', a_prev, v)
    tri = jnp.arange(sub)[:, None] >= jnp.arange(sub)[None, :]
    dec = _masked_exp(tri[None, None, :, :, None, None], bs[:, :, :, None] - bs[:, :, None, :])
    a_diag = jnp.einsum('bithk,biuhk,bituhk->bituh', qs, ks, dec)
    o_diag = jnp.einsum('bituh,biuhv->bithv', a_diag, vs)
    o = o_state + (o_prev + o_diag).reshape(B, C, H, -1)
    b_last = b[:, -1]
    new_state = jnp.exp(b_last)[..., None] * state + jnp.einsum(
        'bchk,bchv->bhkv', k * jnp.exp(b_last[:, None] - b), v)
    return new_state, o


def _hgrn2(xn, w_in, lb, out_gain, w_out, state0, chunk, sub):
    B, T, _ = xn.shape
    q, f, i, og = jnp.split((xn @ w_in).astype(jnp.float32), 4, axis=-1)
    heads = lambda a, d: a.reshape(B, T, A_HEADS, d)
    q = heads(jax.nn.silu(q), A_KDIM)
    log_f = heads(jax.nn.log_sigmoid(f) + jnp.log1p(lb * jnp.exp(jnp.minimum(-f, EXP_CLAMP))), A_KDIM)
    k = heads((1.0 - lb) * jax.nn.sigmoid(-f), A_KDIM)
    v = heads(i, A_VDIM)
    n = T // chunk
    to_chunks = lambda a: a.reshape(B, n, chunk, *a.shape[2:]).swapaxes(0, 1)
    s_fin, o = lax.scan(functools.partial(_gla_chunk, sub=sub), state0.astype(jnp.float32),
                        (to_chunks(q), to_chunks(k), to_chunks(v), to_chunks(log_f)))
    o = o.swapaxes(0, 1).reshape(B, T, A_HEADS, A_VDIM)
    o = _head_rmsnorm(o, out_gain.reshape(A_HEADS, A_VDIM)) * heads(jax.nn.sigmoid(og), A_VDIM)
    return o.reshape(B, T, D_MODEL).astype(xn.dtype) @ w_out, s_fin


def _softmax_av(s, mask, v, eq):
    s = jnp.where(mask, s, NEG)
    m = jnp.max(s, axis=-1, keepdims=True)
    p = jnp.where(mask, jnp.exp(s - m), 0.0)
    den = jnp.sum(p, axis=-1, keepdims=True)
    o = jnp.einsum(eq, p, v) / den
    return o, (m + jnp.log(den))[..., 0]


def _dilated_prompt(q, k, v, dil):
    B, S, Hq, Dh = q.shape
    G = k.shape[2]
    R = Hq // G
    L = S // dil
    P = DIL_KEYS
    nb = -(-L // P)
    by_res = lambda a: a.reshape(B, L, dil, *a.shape[2:]).swapaxes(1, 2)
    qr = jnp.pad(by_res(q), ((0, 0), (0, 0), (0, nb * P - L), (0, 0), (0, 0))).reshape(B, dil, nb, P, G, R, Dh)

    def key_windows(a):
        ap = jnp.pad(by_res(a), ((0, 0), (0, 0), (P, nb * P - L), (0, 0), (0, 0))).reshape(B, dil, nb + 1, P, G, Dh)
        return jnp.concatenate([ap[:, :, :-1], ap[:, :, 1:]], axis=3)

    kw, vw = key_windows(k), key_windows(v)
    s = jnp.einsum('bdnqgrk,bdnsgk->bdngrqs', qr, kw) * HEAD_DIM ** -0.5
    qq = jnp.arange(P)[:, None]
    kk = jnp.arange(2 * P)[None, :]
    band = (kk >= qq) & (kk <= qq + P)
    real = (jnp.arange(nb)[:, None, None] * P + kk[None] - P) >= 0
    mask = (band[None] & real)[None, None, :, None, None]
    o, lse = _softmax_av(s, mask, vw, 'bdngrqs,bdnsgk->bdngrqk')
    o = o.transpose(0, 1, 2, 5, 3, 4, 6).reshape(B, dil, nb * P, Hq, Dh)[:, :, :L]
    lse = lse.transpose(0, 1, 2, 5, 3, 4).reshape(B, dil, nb * P, Hq)[:, :, :L]
    return o.swapaxes(1, 2).reshape(B, S, Hq, Dh), lse.swapaxes(1, 2).reshape(B, S, Hq)


def _dilated_sample(q, k, v, dil, n_past):
    B, T, Hq, Dh = q.shape
    G = k.shape[2]
    R = Hq // G
    idx = n_past + jnp.arange(T)[:, None] - dil * jnp.arange(DIL_KEYS + 1)[None, :]
    valid = idx >= 0
    idx = jnp.maximum(idx, 0)
    kg, vg = k[:, idx], v[:, idx]
    s = jnp.einsum('btgrk,btjgk->btgrj', q.reshape(B, T, G, R, Dh), kg) * HEAD_DIM ** -0.5
    o, lse = _softmax_av(s, valid[None, :, None, None, :], vg, 'btgrj,btjgk->btgrk')
    return o.reshape(B, T, Hq, Dh), lse.reshape(B, T, Hq)


def _trunk(x, pos, hgrn_state0, kv_past, chunk, sub, params):
    (a_norm, a_w_in, a_lb_logits, a_out_norm, a_w_out, kv_norm, w_kv, k_norm,
     b_norm, b_w_q, q_norm, b_w_o, mlp_norm, mlp_w_up, mlp_w_down) = params
    B, T, _ = x.shape
    sm = jax.nn.softmax(a_lb_logits.astype(jnp.float32), axis=0)
    lbs = jnp.cumsum(sm, axis=0) - sm[0]
    finals = []
    kv_new, attn_kv = None, None
    for layer in range(DEPTH):
        if layer < N_A_LAYERS:
            h, s_fin = _hgrn2(_rmsnorm(x, a_norm[layer]), a_w_in[layer], lbs[layer], a_out_norm[layer],
                              a_w_out[layer], hgrn_state0[layer], chunk, sub)
            finals.append(s_fin)
        else:
            if layer == N_A_LAYERS:
                kv = (_rmsnorm(x, kv_norm) @ w_kv).astype(jnp.float32).reshape(B, T, N_GROUPS, 2, KV_HEADS, HEAD_DIM)
                k = _rotary(_head_rmsnorm(kv[:, :, :, 0], k_norm[:, None, :]), pos)
                v = kv[:, :, :, 1]
                kv_new = [jnp.stack([k[:, :, g], v[:, :, g]], axis=2) for g in range(N_GROUPS)]
                if kv_past is None:
                    attn_kv = kv_new
                else:
                    attn_kv = [jnp.concatenate([kv_past[g].astype(jnp.float32), kv_new[g]], axis=1)
                               for g in range(N_GROUPS)]
            j = layer - N_A_LAYERS
            q = (_rmsnorm(x, b_norm[j]) @ b_w_q[j]).astype(jnp.float32).reshape(B, T, N_GROUPS, Q_HEADS, HEAD_DIM)
            q = _rotary(_head_rmsnorm(q, q_norm[j][:, None, :]), pos)
            outs, lses = [], []
            for g in range(N_GROUPS):
                kg, vg = attn_kv[g][:, :, 0], attn_kv[g][:, :, 1]
                if kv_past is None:
                    o, l = _dilated_prompt(q[:, :, g], kg, vg, DILATIONS[g])
                else:
                    o, l = _dilated_sample(q[:, :, g], kg, vg, DILATIONS[g], kv_past[g].shape[1])
                outs.append(o)
                lses.append(l)
            w = jax.nn.softmax(jnp.stack(lses), axis=0)
            o = jnp.sum(w[..., None] * jnp.stack(outs), axis=0).reshape(B, T, Q_HEADS * HEAD_DIM)
            h = o.astype(x.dtype) @ b_w_o[j]
        x = x + h.astype(x.dtype)
        x = x + _sqrelu_mlp(_rmsnorm(x, mlp_norm[layer]), mlp_w_up[layer], mlp_w_down[layer]).astype(x.dtype)
    return x, jnp.stack(finals), kv_new


def setup_inputs(seed: int = 0) -> dict:
    key = jax.random.key(seed)
    ks = jax.random.split(key, 21)

    def nrm(k, shape, scale=1.0):
        return jax.random.normal(k, shape, jnp.float32) * scale

    def gain(k, shape):
        return 1.0 + 0.05 * jax.random.normal(k, shape, jnp.float32)

    kv_row = (2, KV_HEADS, HEAD_DIM)
    return {
        'x_prompt': nrm(ks[0], (BATCH, SEQ, D_MODEL)),
        'x_sample': nrm(ks[1], (DEC_BATCH, DEC_SEQ, D_MODEL)),
        'state_hgrn': nrm(ks[2], (N_A_LAYERS, DEC_BATCH, A_HEADS, A_KDIM, A_VDIM), 0.5),
        'cache_win1_kv': nrm(ks[3], (DEC_BATCH, min(WINDOWS[0], PAST_LEN)) + kv_row),
        'cache_win2_kv': nrm(ks[4], (DEC_BATCH, min(WINDOWS[1], PAST_LEN)) + kv_row),
        'cache_win3_kv': nrm(ks[5], (DEC_BATCH, min(WINDOWS[2], PAST_LEN)) + kv_row),
        'a_norm': gain(ks[6], (N_A_LAYERS, D_MODEL)),
        'a_w_in': nrm(ks[7], (N_A_LAYERS, D_MODEL, 4 * D_MODEL), D_MODEL ** -0.5),
        'a_lb_logits': nrm(ks[8], (N_A_LAYERS, D_MODEL), 0.5),
        'a_out_norm': gain(ks[9], (N_A_LAYERS, D_MODEL)),
        'a_w_out': nrm(ks[10], (N_A_LAYERS, D_MODEL, D_MODEL), D_MODEL ** -0.5),
        'kv_norm': gain(ks[11], (D_MODEL,)),
        'w_kv': nrm(ks[12], (D_MODEL, N_GROUPS * 2 * KV_HEADS * HEAD_DIM), D_MODEL ** -0.5),
        'k_norm': gain(ks[13], (N_GROUPS, HEAD_DIM)),
        'b_norm': gain(ks[14], (N_B_LAYERS, D_MODEL)),
        'b_w_q': nrm(ks[15], (N_B_LAYERS, D_MODEL, N_GROUPS * Q_HEADS * HEAD_DIM), D_MODEL ** -0.5),
        'q_norm': gain(ks[16], (N_B_LAYERS, N_GROUPS, HEAD_DIM)),
        'b_w_o': nrm(ks[17], (N_B_LAYERS, Q_HEADS * HEAD_DIM, D_MODEL), (Q_HEADS * HEAD_DIM) ** -0.5),
        'mlp_norm': gain(ks[18], (DEPTH, D_MODEL)),
        'mlp_w_up': nrm(ks[19], (DEPTH, D_MODEL, D_FF), D_MODEL ** -0.5),
        'mlp_w_down': nrm(ks[20], (DEPTH, D_FF, D_MODEL), D_FF ** -0.5),
    }


def reference(x_prompt, x_sample, state_hgrn, cache_win1_kv, cache_win2_kv, cache_win3_kv,
              a_norm, a_w_in, a_lb_logits, a_out_norm, a_w_out, kv_norm, w_kv, k_norm,
              b_norm, b_w_q, q_norm, b_w_o, mlp_norm, mlp_w_up, mlp_w_down):
    params = (a_norm, a_w_in, a_lb_logits, a_out_norm, a_w_out, kv_norm, w_kv, k_norm,
              b_norm, b_w_q, q_norm, b_w_o, mlp_norm, mlp_w_up, mlp_w_down)
    bp, tp, _ = x_prompt.shape
    ts = x_sample.shape[1]
    zero_state = jnp.zeros((N_A_LAYERS, bp, A_HEADS, A_KDIM, A_VDIM), jnp.float32)
    y_prompt, state_hgrn_prompt, kv_p = _trunk(x_prompt, jnp.arange(tp), zero_state, None,
                                               A_CHUNK, A_SUB, params)
    y_sample, state_hgrn_sample, kv_s = _trunk(x_sample, PAST_LEN + jnp.arange(ts), state_hgrn,
                                               (cache_win1_kv, cache_win2_kv, cache_win3_kv),
                                               ts, ts, params)
    win_p = [kv_p[g][:, max(tp - WINDOWS[g], 0):] for g in range(N_GROUPS)]
    return (y_prompt, y_sample, state_hgrn_prompt, state_hgrn_sample,
            win_p[0], win_p[1], win_p[2], kv_s[0], kv_s[1], kv_s[2])
```

```python
import contextlib
import numpy as np
import ml_dtypes
import concourse.bass as bass
import concourse.mybir as mybir
from concourse.bass_utils import run_bass_kernel_spmd

F32 = mybir.dt.float32
BF16 = mybir.dt.bfloat16
AF = mybir.ActivationFunctionType
ALU = mybir.AluOpType
AX = mybir.AxisListType

T = 2048
D = 1024
NSQ = 4
NST = 16
EPS = 1e-6
PAST = 8192
THETA = 500000.0
WINS = (128, 512, 2048)
DILS = (1, 4, 16)
SCALE = 128 ** -0.5


def ss(start, n, step):
    return slice(start, start + (n - 1) * step + 1, step)


def pipeline(gens, depth=2, lag=1):
    active = []
    it = iter(gens)
    pending = next(it, None)
    while active or pending is not None:
        if pending is not None and len(active) < depth and (not active or active[-1][1] >= lag):
            active.append([pending, 0])
            pending = next(it, None)
        for a in list(active):
            try:
                next(a[0])
                a[1] += 1
            except StopIteration:
                active.remove(a)


class Buf:
    __slots__ = ("t", "w", "r", "dsem", "dcnt", "name")

    def __init__(self, t, name=""):
        self.t = t
        self.w = None
        self.r = {}
        self.dsem = None
        self.dcnt = 0
        self.name = name

    def __getitem__(self, k):
        return self.t[k]


class Prog:
    def __init__(self, nc):
        self.nc = nc
        self.eng = {"pe": nc.tensor, "act": nc.scalar, "dve": nc.vector, "pool": nc.gpsimd, "sp": nc.sync}
        self.sem = {e: nc.alloc_semaphore("s_" + e) for e in ("pe", "act", "dve", "pool")}
        self.cnt = {e: 0 for e in self.sem}
        self.seen = {e: {} for e in self.eng}
        self.semobj = {}
        self.dma_bufs = []
        self.n_ins = 0
        self.free_sems = []
        self.nsem = 0

    def _key(self, sem):
        k = id(sem)
        self.semobj[k] = sem
        return k

    def _deps(self, reads, writes):
        need = {}
        for b in reads:
            if b.w is not None:
                k, v = b.w
                need[k] = max(need.get(k, 0), v)
        for b in writes:
            if b.w is not None:
                k, v = b.w
                need[k] = max(need.get(k, 0), v)
            for k, v in b.r.items():
                need[k] = max(need.get(k, 0), v)
        return need

    def _wait(self, e, need):
        own = self._key(self.sem["pe"]) if e == "pe" else None
        for k, v in need.items():
            if k == own:
                continue
            if self.seen[e].get(k, 0) >= v:
                continue
            self.eng[e].wait_ge(self.semobj[k], v)
            self.seen[e][k] = v

    def _mark(self, ev, reads, writes):
        k, v = ev
        for b in reads:
            b.r[k] = max(b.r.get(k, 0), v)
        for b in writes:
            b.w = ev
            b.r = {}

    def op(self, e, fn, reads=(), writes=()):
        self._wait(e, self._deps(reads, writes))
        ins = fn(self.eng[e])
        self.cnt[e] += 1
        ins.then_inc(self.sem[e], 1)
        self._mark((self._key(self.sem[e]), self.cnt[e]), reads, writes)
        self.n_ins += 1

    def dma(self, q, out, in_, reads=(), writes=(), sb=None, slow=False, chain=True):
        if sb is None:
            sb = writes[0]
        if sb.dsem is None:
            if self.free_sems:
                sb.dsem, sb.dcnt = self.free_sems.pop()
            else:
                self.nsem += 1
                sb.dsem = self.nc.alloc_semaphore("d%d" % self.nsem)
            self.dma_bufs.append(sb)
        need = self._deps(reads, writes)
        k = self._key(sb.dsem)
        if sb.dcnt and chain:
            need[k] = max(need.get(k, 0), sb.dcnt)
        self._wait(q, need)
        ins = self.eng[q].dma_start(out=out, in_=in_, allow_slow_non_contiguous=slow)
        sb.dcnt += 16
        ins.then_inc(sb.dsem, 16)
        self._mark((k, sb.dcnt), reads, writes)
        self.n_ins += 1

    def collective(self, ins_ap, outs_ap, groups, reads, writes):
        if not hasattr(self, "ccsem"):
            self.ccsem = self.nc.alloc_semaphore("ccsem")
            self.cccnt = 0
            self.ccbuf = Buf(None, "cc")
            self.ccbuf.dsem = self.ccsem
            self.dma_bufs.append(self.ccbuf)
        need = self._deps(reads, writes)
        k = self._key(self.ccsem)
        if self.cccnt:
            need[k] = self.cccnt
        self._wait("pool", need)
        ins = self.eng["pool"].collective_compute("AllGather", ALU.bypass, replica_groups=groups, ins=[ins_ap], outs=[outs_ap])
        self.cccnt += 1
        self.ccbuf.dcnt = self.cccnt
        ins.then_inc(self.ccsem)
        self._mark((k, self.cccnt), reads, writes)

    def release(self, bufs):
        for b in bufs:
            if b.dsem is not None and b in self.dma_bufs:
                self.dma_bufs.remove(b)
                self.free_sems.append((b.dsem, b.dcnt))
                b.dsem = None

    def barrier(self):
        need = {}
        for e in self.sem:
            if self.cnt[e]:
                need[self._key(self.sem[e])] = self.cnt[e]
        for b in self.dma_bufs:
            if b.dcnt:
                need[self._key(b.dsem)] = b.dcnt
        for e in self.eng:
            self._wait(e, dict(need))


def build(mode="full"):
    nc = bass.Bass("TRN2", target_bir_lowering=False)
    P = Prog(nc)

    def din(name, shape, dt=F32):
        return nc.dram_tensor(name, list(shape), dt, kind="ExternalInput").ap()

    def dout(name, shape, dt=F32):
        return nc.dram_tensor(name, list(shape), dt, kind="ExternalOutput").ap()

    def dscr(name, shape, dt=F32):
        return nc.dram_tensor(name, list(shape), dt, kind="Internal").ap()

    xp = din("xp", [T, D])
    xsm = din("xsm", [NST, D])
    st_in = din("st_in", [2, NSQ, 8, 128, 128])
    cache = [din("c1", [NSQ, 128, 512]), din("c2", [NSQ, 512, 512]), din("c3", [NSQ, 2048, 512])]
    a_norm = din("a_norm", [2, D])
    a_w_in = din("a_w_in", [2, D, 4 * D])
    a_lb = din("a_lb_logits", [2, D])
    a_out_norm = din("a_out_norm", [2, D])
    a_w_out = din("a_w_out", [2, D, D])
    kv_norm = din("kv_norm", [D])
    w_kv = din("w_kv", [D, 1536])
    k_norm = din("k_norm", [3, 128])
    b_norm = din("b_norm", [2, D])
    b_w_q = din("b_w_q", [2, D, 3072])
    q_norm = din("q_norm", [2, 3, 128])
    b_w_o = din("b_w_o", [2, D, D])
    mlp_norm = din("mlp_norm", [4, D])
    mlp_w_up = din("mlp_w_up", [4, D, 4 * D])
    mlp_w_down = din("mlp_w_down", [4, 4 * D, D])
    c_cos = din("c_cos", [T, 16])
    c_sin = din("c_sin", [T, 16])
    c_cos_s = din("c_cos_s", [NST, 16])
    c_sin_s = din("c_sin_s", [NST, 16])
    c_ident = din("c_ident", [128, 128])
    c_masks = din("c_masks", [128, 5, 128])
    c_scan = din("c_scan", [2, 2048])
    c_rowm = din("c_rowm", [128, 4])
    c_mbias = din("c_mbias", [128, 3, 512])
    c_flag = din("c_flag", [128, 1])

    yp = dout("yp", [T, D])
    ysm = dout("ysm", [NST, D])
    stp = dout("stp", [2, 8, 128, 128])
    sts = dout("sts", [2, NSQ, 8, 128, 128])
    winp = [dout("w1p", [128, 512]), dout("w2p", [512, 512]), dout("w3p", [2048, 512])]
    kvs = dout("kvs", [NST, 1536])

    xs = dscr("xs", [T, D])
    xss = dscr("xss", [NST, D])
    qp = dscr("qp", [T, 3072], BF16)
    qs = dscr("qs", [NST, 3072], BF16)
    st_x = nc.dram_tensor("st_x", [1024, 128], F32)
    st_g = nc.dram_tensor("st_g", [2048, 128], F32)
    kvp_t = nc.dram_tensor("kvp", [T, 1536], F32)
    kvb16 = dscr("kvb16", [T, 1536], BF16)
    B_kvb = Buf(None, "kvb16")
    WX = [(0, 0, 128), (1, 0, 512), (2, 0, 1024), (2, 1024, 1024)]
    wx_t = [nc.dram_tensor("wx%d" % i, [n_, 512], F32) for i, (g_, o_, n_) in enumerate(WX)]
    wg_t = [nc.dram_tensor("wg%d" % i, [2 * n_, 512], F32) for i, (g_, o_, n_) in enumerate(WX)]
    B_wx = [Buf(None, "wx%d" % i) for i in range(4)]
    wgb = [dscr("wgb%d" % g, [WINS[g], 512], BF16) for g in range(3)]
    cb16 = [dscr("cb16_%d" % g, [NSQ * WINS[g], 512], BF16) for g in range(3)]
    kvs16 = dscr("kvs16", [NST, 1536], BF16)
    B_cb = Buf(None, "cb16")
    B_wgb = Buf(None, "wgb")
    B_stx = Buf(None, "stx")
    B_stg = Buf(None, "stg")
    B_kvg = Buf(None, "kvg")
    GROUPS = [[0, 1], [2, 3], [4, 5], [6, 7]]

    B_xs = [Buf(None, "xs%d" % i) for i in range(T // 128)]
    B_xss = Buf(None, "xss")
    B_kvp = Buf(None, "kvp")
    B_kvs = Buf(None, "kvs")
    B_qp = Buf(None, "qp")
    B_qs = Buf(None, "qs")
    B_out = Buf(None, "out")
    kvp = kvp_t.ap()

    stack = contextlib.ExitStack()

    uniq = [0]

    phase_bufs = []

    def sb(name, shape, dt=F32, st=None):
        uniq[0] += 1
        t = (st or stack).enter_context(nc.sbuf_tensor("%s_%d" % (name, uniq[0]), list(shape), dt))
        b = Buf(t, name)
        if st is not None:
            phase_bufs.append(b)
        return b

    def end_phase():
        P.barrier()
        P.release(phase_bufs)
        del phase_bufs[:]

    ident = sb("ident", [128, 128], BF16)
    masks = sb("masks", [128, 5, 128], BF16)
    scanm = sb("scanm", [128, 2, 256], BF16)
    ones = sb("ones", [128, 128], BF16)
    P.dma("pool", ident[:], c_ident, writes=[ident])
    P.dma("pool", masks[:], c_masks, writes=[masks])
    P.dma("pool", scanm[:], c_scan[:, 0:256].partition_broadcast(128), writes=[scanm])
    P.op("dve", lambda e: e.memset(ones[:], 1.0), writes=[ones])

    psf = []
    for i in range(4):
        psf.append(Buf(stack.enter_context(nc.psum_tensor("psf%d" % i, [128, 512], F32)), "psf%d" % i))
    pso = [Buf(stack.enter_context(nc.psum_tensor("pso%d" % i, [128, 512], F32)), "pso%d" % i) for i in range(2)]
    psb = []
    for i in range(2):
        psb.append(Buf(stack.enter_context(nc.psum_tensor("psb%d" % i, [128, 1024], BF16)), "psb%d" % i))
    ring = {"f": 0, "b": 0}

    ringbanks = {"l": psf + pso}

    def PSF():
        ring["f"] = (ring["f"] + 1) % len(ringbanks["l"])
        return ringbanks["l"][ring["f"]]

    def PSB():
        ring["b"] = (ring["b"] + 1) % len(psb)
        return psb[ring["b"]]

    def mm(out_ap, lhsT, rhs, start, stop, reads, writes):
        P.op("pe", lambda e: e.matmul(out_ap, lhsT=lhsT, rhs=rhs, start=start, stop=stop), reads=reads, writes=writes)

    def tr(out_ap, in_ap, n, reads, writes):
        P.op("pe", lambda e: e.transpose(out_ap, in_ap, ident[0:n, 0:n]), reads=list(reads) + [ident], writes=writes)

    def load_w(dst, src_ap, kc, fdim, nsplit):
        v = src_ap.rearrange("(c p) f -> p c f", p=128)
        step = kc // nsplit
        for i in range(nsplit):
            P.dma("pool", dst[:, i * step:(i + 1) * step, :], v[:, i * step:(i + 1) * step, :], writes=[dst], chain=(i == 0))

    def rstd_from_ssq(out_ap, in_ap, n, bufs_r, bufs_w):
        P.op("act", lambda e: e.activation(out=out_ap, in_=in_ap, func=AF.Ln, scale=1.0 / n, bias=EPS), reads=bufs_r, writes=bufs_w)
        P.op("act", lambda e: e.activation(out=out_ap, in_=out_ap, func=AF.Exp, scale=-0.5), reads=bufs_w, writes=bufs_w)

    class NormCtx:
        def __init__(self, st, tt, gain_ap, nxt=2, nT=2):
            self.tt = tt
            nb = max(1, tt // 128)
            self.gain = sb("gain", [128, D], F32, st)
            P.dma("sp", self.gain[:], gain_ap.partition_broadcast(128), writes=[self.gain])
            self.xt = [sb("xt%d" % i, [128, nb, D], F32, st) for i in range(nxt)]
            self.xn = [sb("xn%d" % i, [128, D], BF16, st) for i in range(2)]
            self.sq = sb("sqj", [128, D], BF16, st)
            self.ssq = [sb("ssq%d" % i, [128, 2], F32, st) for i in range(2)]
            self.xnT = [sb("xnT%d" % i, [128, 8, tt], BF16, st) for i in range(nT)]
            self.i = 0
            self.li = 0

        def load(self, src_ap, n, src_bufs):
            self.li = (self.li + 1) % len(self.xt)
            xt = self.xt[self.li]
            bs = min(128, n)
            nb = n // bs
            P.dma("act", xt[0:bs, 0:nb, :], src_ap.rearrange("(b p) f -> p b f", p=bs), reads=src_bufs, writes=[xt])
            return xt

        def norm(self, xt, n):
            self.i = (self.i + 1) % len(self.xnT)
            xnT = self.xnT[self.i]
            bs = min(128, n)
            nb = n // bs
            for b in range(nb):
                ssq = self.ssq[b & 1]
                xn = self.xn[b & 1]
                P.op("act", lambda e: e.activation(out=self.sq[0:bs, :], in_=xt[0:bs, b, :], func=AF.Square, accum_out=ssq[0:bs, 0:1]),
                     reads=[xt], writes=[self.sq, ssq])
                rstd_from_ssq(ssq[0:bs, 1:2], ssq[0:bs, 0:1], D, [ssq], [ssq])
                P.op("dve", lambda e: e.scalar_tensor_tensor(out=xn[0:bs, :], in0=xt[0:bs, b, :], scalar=ssq[0:bs, 1:2], in1=self.gain[0:bs, :],
                                                            op0=ALU.mult, op1=ALU.mult), reads=[xt, ssq, self.gain], writes=[xn])
                pb = PSB()
                for c in range(8):
                    tr(pb[:, c * 128:c * 128 + bs], xn[0:bs, c * 128:(c + 1) * 128], bs, [xn], [pb])
                P.op("act", lambda e: e.copy(out=xnT[:, :, b * bs:(b + 1) * bs], in_=pb[:, :].rearrange("p (c t) -> p c t", c=8)[:, :, 0:bs]),
                     reads=[pb], writes=[xnT])
            return xnT

        def run(self, src_ap, n, src_bufs):
            xt = self.load(src_ap, n, src_bufs)
            return xt, self.norm(xt, n)

    def tile_list(tt):
        return [(False, i * tt, tt) for i in range(T // tt)] + [(True, 0, NST)]

    def xbufs(smp, s0, n):
        return [B_xss] if smp else B_xs[s0 // 128:(s0 + n) // 128]

    def mlp_phase(l, src_p, src_s, dst_p, dst_s, first, last):
        TT = 256
        with contextlib.ExitStack() as st:
            w_up = [sb("w_up%d" % i, [128, 8, 512], BF16, st) for i in range(8)]
            w_dn = [sb("w_dn%d" % i, [128, 4, D], BF16, st) for i in range(8)]
            vu = mlp_w_up[l].rearrange("(c p) f -> p c f", p=128)
            vd = mlp_w_down[l].rearrange("(c p) f -> p c f", p=128)
            for i in range(8):
                P.dma("pool", w_up[i][:], vu[:, :, i * 512:(i + 1) * 512], writes=[w_up[i]])
            for i in range(8):
                P.dma("pool", w_dn[i][:], vd[:, i * 4:(i + 1) * 4, :], writes=[w_dn[i]])
            N = NormCtx(st, TT, mlp_norm[l])
            hT = [sb("hT%d" % i, [128, 32, TT], BF16, st) for i in range(2)]
            rl = [sb("rl%d" % i, [128, TT], F32, st) for i in range(2)]

            def tile_gen(ti, smp, s0, n):
                src = (src_s if smp else src_p[s0:s0 + n, :])
                dst = (dst_s if smp else dst_p[s0:s0 + n, :])
                rb = [] if first else xbufs(smp, s0, n)
                wb = [B_out] if last else xbufs(smp, s0, n)
                xt, xnT = N.run(src, n, rb)
                yield
                h = hT[ti & 1]
                bs = min(128, n)
                nb = n // bs
                for fc in range(32):
                    ps = PSF()
                    for kc in range(8):
                        mm(ps[:, 0:n], w_up[fc // 4][:, kc, (fc % 4) * 128:(fc % 4 + 1) * 128], xnT[:, kc, 0:n], kc == 0, kc == 7, [w_up[fc // 4], xnT], [ps])
                    r = rl[fc & 1]
                    P.op("act", lambda e: e.activation(out=r[:, 0:n], in_=ps[:, 0:n], func=AF.Relu), reads=[ps], writes=[r])
                    P.op("dve", lambda e: e.tensor_tensor(out=h[:, fc, 0:n], in0=r[:, 0:n], in1=r[:, 0:n], op=ALU.mult),
                         reads=[r], writes=[h])
                    if fc & 1:
                        yield
                for b in range(nb):
                    for hf in range(2):
                        ps = PSF()
                        for fc in range(32):
                            mm(ps[0:bs, :], h[:, fc, b * bs:(b + 1) * bs], w_dn[fc // 4][:, fc % 4, hf * 512:(hf + 1) * 512], fc == 0, fc == 31, [h, w_dn[fc // 4]], [ps])
                        P.op("dve", lambda e: e.tensor_tensor(out=xt[0:bs, b, hf * 512:(hf + 1) * 512], in0=ps[0:bs, :], in1=xt[0:bs, b, hf * 512:(hf + 1) * 512], op=ALU.add),
                             reads=[ps, xt], writes=[xt])
                        yield
                P.dma("sp", dst.rearrange("(b p) f -> p b f", p=bs), xt[0:bs, 0:nb, :], reads=[xt], writes=wb, sb=xt)

            pipeline((tile_gen(ti, *t) for ti, t in enumerate(tile_list(TT))), depth=2, lag=10)
        end_phase()

    rowm = sb("rowm", [128, 4], F32)
    P.dma("sp", rowm[:], c_rowm, writes=[rowm])
    flag = sb("flag", [128, 1], F32)
    P.dma("sp", flag[:], c_flag, writes=[flag])

    def hgrn_phase(l, src_p, src_s, dst_p, dst_s, first, state_only=False, wp=None):
        TT = 256
        so = state_only
        ringbanks["l"] = psf
        with contextlib.ExitStack() as st:
            def wcol(kind, h_):
                p_ = wp[kind + str(h_ // 4)]
                return p_, p_[:, :, (h_ % 4) * 128:(h_ % 4 + 1) * 128]

            if so:
                w_out = None
            else:
                w_out = sb("w_out", [128, 8, D], BF16, st)
                load_w(w_out, a_w_out[l], 8, D, 2)
            N = NormCtx(st, TT, a_norm[l], nxt=3 if so else 2, nT=3 if so else 2)
            oml = sb("oml", [128, 8], F32, st)
            lbt = sb("lbt", [128, 2, 8], F32, st)
            ogain = sb("ogain", [128, 8], F32, st)
            P.dma("sp", lbt[:], a_lb.rearrange("l (h k) -> k l h", k=128), writes=[lbt], slow=True)
            P.dma("sp", ogain[:], a_out_norm[l].rearrange("(h k) -> k h", k=128), writes=[ogain], slow=True)
            if l == 0:
                P.op("dve", lambda e: e.memset(oml[:], 1.0), writes=[oml])
            else:
                P.op("dve", lambda e: e.tensor_tensor(out=oml[:], in0=lbt[:, 0, :], in1=lbt[:, 1, :], op=ALU.subtract), reads=[lbt], writes=[oml])
                P.op("act", lambda e: e.activation(out=oml[:], in_=oml[:], func=AF.Sigmoid), reads=[oml], writes=[oml])
            NB2 = 3 if so else 2
            NA = 2 if so else 1
            qf = None if so else sb("qf", [128, 8, TT], BF16, st)
            kfL = [sb("kf%d" % i, [128, 8, TT], F32, st) for i in range(NA)]
            glL = [sb("gl%d" % i, [128, 8, TT], F32, st) for i in range(NA)]
            eeL = [sb("ee%d" % i, [128, 8, TT], F32, st) for i in range(NA)]
            refcL = [sb("refc%d" % i, [128, 8, 4], F32, st) for i in range(NA)]
            QT = [None] if so else [sb("QT%d" % i, [128, 8, TT], BF16, st) for i in range(2)]
            KT = [sb("KT%d" % i, [128, 8, TT], BF16, st) for i in range(NB2)]
            KhT = [sb("KhT%d" % i, [128, 8, TT], BF16, st) for i in range(NB2)]
            sgo = [None] if so else [sb("sgo%d" % i, [128, 8, TT], BF16, st) for i in range(2)]
            er = [sb("er%d" % i, [128, 8, 4], F32, st) for i in range(NB2)]
            ebr = [sb("ebr%d" % i, [128, 8, 4], F32, st) for i in range(NB2)]
            ebl = [sb("ebl%d" % i, [128, 8, 4], F32, st) for i in range(NB2)]
            vt = [sb("vt%d" % i, [128, D], BF16, st) for i in range(2)]
            kt = [sb("kt%d" % i, [128, D], BF16, st) for i in range(2)]
            ktm = [sb("ktm%d" % i, [128, D], BF16, st) for i in range(1)]
            at = [None, None] if so else [sb("at%d" % i, [128, 4, 128], BF16, st) for i in range(2)]
            S = sb("S", [128, 8, 128], F32, st)
            Sb = None if so else sb("Sb", [128, 8, 128], BF16, st)
            sqo = None if so else sb("sqo", [128, 512], BF16, st)
            rs = None if so else sb("rs", [128, 512], F32, st)
            ogT = None if so else sb("ogT", [128, 8, TT], BF16, st)
            if so:
                P.op("dve", lambda e: e.memset(S[:], 0.0), writes=[S])
                tiles = tile_list(TT)[:-1]
            else:
                P.dma("sp", S[:], st_g[0:1024, :].rearrange("(h k) v -> k h v", k=128), reads=[B_stg], writes=[S])
                P.op("dve", lambda e: e.tensor_scalar(out=S[:].rearrange("p h v -> p (h v)"), in0=S[:].rearrange("p h v -> p (h v)"), scalar1=flag[:, 0:1], scalar2=None, op0=ALU.mult),
                     reads=[S, flag], writes=[S])
                tiles = tile_list(TT)
            bic = [0]
            doneA, doneB, tctx = {}, {}, {}

            def tparams(ti):
                smp, s0, n = tiles[ti]
                C = 4 if smp else 64
                bs = min(128, n)
                p2 = ti % NB2
                pq = 0 if so else (ti & 1)
                return dict(smp=smp, s0=s0, n=n, C=C, bs=bs, nb=n // bs, cpb=bs // C, nch=n // C, mi=3 if smp else 0,
                            QT_=QT[pq], KT_=KT[p2], KhT_=KhT[p2], sgo_=sgo[pq], er_=er[p2], ebr_=ebr[p2], ebl_=ebl[p2])

            def genA(ti):
                tp = tparams(ti)
                smp, s0, n, C, nch = tp["smp"], tp["s0"], tp["n"], tp["C"], tp["nch"]
                QT_, KT_, KhT_, sgo_, er_, ebr_, ebl_ = tp["QT_"], tp["KT_"], tp["KhT_"], tp["sgo_"], tp["er_"], tp["ebr_"], tp["ebl_"]
                kf, gl, ee, refc = kfL[ti % NA], glL[ti % NA], eeL[ti % NA], refcL[ti % NA]
                src = (src_s if smp else src_p[s0:s0 + n, :])
                rb = [] if first else xbufs(smp, s0, n)
                xt, xnT = N.run(src, n, rb)
                tctx[ti] = (xt, xnT)
                yield
                for h in range(8):
                    ps = PSF()
                    for kc in range(8):
                        mm(ps[:, 0:n], wcol('f', h)[1][:, kc, :], xnT[:, kc, 0:n], kc == 0, kc == 7, [wcol('f', h)[0], xnT], [ps])
                    P.op("act", lambda e: e.activation(out=kf[:, h, 0:n], in_=ps[:, 0:n], func=AF.Sigmoid, scale=-1.0), reads=[ps], writes=[kf])
                    yield
                kv_ = kf[:, :, 0:n]
                gv_ = gl[:, :, 0:n]
                ev_ = ee[:, :, 0:n]
                P.op("dve", lambda e: e.tensor_tensor(out=kv_, in0=kv_, in1=oml[:, :].unsqueeze(2).broadcast_to([128, 8, n]), op=ALU.mult), reads=[kf, oml], writes=[kf])
                if not so:
                    for h in range(8):
                        ps = PSF()
                        for kc in range(8):
                            mm(ps[:, 0:n], wcol('og', h)[1][:, kc, :], xnT[:, kc, 0:n], kc == 0, kc == 7, [wcol('og', h)[0], xnT], [ps])
                        P.op("act", lambda e: e.activation(out=sgo_[:, h, 0:n], in_=ps[:, 0:n], func=AF.Sigmoid), reads=[ps], writes=[sgo_])
                        yield
                P.op("act", lambda e: e.activation(out=gv_, in_=kv_, func=AF.Ln, scale=-1.0, bias=1.0), reads=[kf], writes=[gl])
                yield

                def qproj(h):
                    if so:
                        return
                    ps = PSF()
                    for kc in range(8):
                        mm(ps[:, 0:n], wcol('q', h)[1][:, kc, :], xnT[:, kc, 0:n], kc == 0, kc == 7, [wcol('q', h)[0], xnT], [ps])
                    P.op("act", lambda e: e.activation(out=qf[:, h, 0:n], in_=ps[:, 0:n], func=AF.Silu), reads=[ps], writes=[qf])

                for h in range(8):
                    P.op("dve", lambda e: e.tensor_tensor_scan(out=gl[:, h, 0:n], data0=scanm[:, 1 if smp else 0, 0:n], data1=gl[:, h, 0:n], initial=0.0,
                                                              op0=ALU.mult, op1=ALU.add), reads=[gl, scanm], writes=[gl])
                    qproj(h)
                    yield
                g4 = gv_.rearrange("p h (c t) -> p h c t", t=C)
                P.op("dve", lambda e: e.tensor_copy(out=refc[:, :, 0:nch], in_=g4[:, :, :, C // 2 - 1:C // 2].rearrange("p h c o -> p h (c o)")), reads=[gl], writes=[refc])
                P.op("dve", lambda e: e.tensor_tensor(out=g4, in0=g4, in1=refc[:, :, 0:nch].unsqueeze(3).broadcast_to([128, 8, nch, C]), op=ALU.subtract),
                     reads=[gl, refc], writes=[gl])
                yield
                P.op("act", lambda e: e.activation(out=er_[:, :, 0:nch], in_=refc[:, :, 0:nch], func=AF.Exp), reads=[refc], writes=[er_])
                P.op("act", lambda e: e.activation(out=ev_, in_=gv_, func=AF.Exp), reads=[gl], writes=[ee])
                yield
                if not so:
                    P.op("pool", lambda e: e.tensor_tensor(out=QT_[:, :, 0:n], in0=qf[:, :, 0:n], in1=ev_, op=ALU.mult), reads=[qf, ee], writes=[QT_])
                e4 = ev_.rearrange("p h (c t) -> p h c t", t=C)
                P.op("dve", lambda e: e.tensor_copy(out=ebr_[:, :, 0:nch], in_=e4[:, :, :, C - 1:C].rearrange("p h c o -> p h (c o)")), reads=[ee], writes=[ebr_])
                yield
                P.op("act", lambda e: e.activation(out=ev_, in_=gv_, func=AF.Exp, scale=-1.0), reads=[gl], writes=[ee])
                P.op("dve", lambda e: e.tensor_tensor(out=KT_[:, :, 0:n], in0=kv_, in1=ev_, op=ALU.mult), reads=[kf, ee], writes=[KT_])
                yield
                P.op("pool", lambda e: e.tensor_tensor(out=KhT_[:, :, 0:n].rearrange("p h (c t) -> p h c t", t=C), in0=KT_[:, :, 0:n].rearrange("p h (c t) -> p h c t", t=C),
                                                       in1=ebr_[:, :, 0:nch].unsqueeze(3).broadcast_to([128, 8, nch, C]), op=ALU.mult), reads=[KT_, ebr_], writes=[KhT_])
                P.op("dve", lambda e: e.tensor_tensor(out=ebl_[:, :, 0:nch], in0=er_[:, :, 0:nch], in1=ebr_[:, :, 0:nch], op=ALU.mult), reads=[er_, ebr_], writes=[ebl_])
                doneA[ti] = True

            def genB(ti):
                while not doneA.get(ti):
                    yield
                while ti > 0 and not doneB.get(ti - 1):
                    yield
                tp = tparams(ti)
                smp, s0, n, C, bs, nb, cpb, mi = tp["smp"], tp["s0"], tp["n"], tp["C"], tp["bs"], tp["nb"], tp["cpb"], tp["mi"]
                QT_, KT_, KhT_, sgo_, er_, ebr_, ebl_ = tp["QT_"], tp["KT_"], tp["KhT_"], tp["sgo_"], tp["er_"], tp["ebr_"], tp["ebl_"]
                xt, xnT = tctx[ti]
                dst = (dst_s if smp else dst_p[s0:s0 + n, :])
                wb = xbufs(smp, s0, n)
                for b in range(nb):
                    bic[0] += 1
                    vt_, kt_ = vt[bic[0] & 1], kt[bic[0] & 1]
                    cols = slice(b * bs, (b + 1) * bs)
                    for hf in range(2):
                        ps = PSF()
                        for kc in range(8):
                            mm(ps[0:bs, :], xnT[:, kc, cols], wp['i' + str(hf)][:, kc, :], kc == 0, kc == 7, [wp['i' + str(hf)], xnT], [ps])
                        P.op("act", lambda e: e.copy(out=vt_[0:bs, hf * 512:(hf + 1) * 512], in_=ps[0:bs, :]), reads=[ps], writes=[vt_])
                        yield
                    pb = PSB()
                    for h in range(8):
                        tr(pb[0:bs, h * 128:(h + 1) * 128], KhT_[:, h, cols], 128, [KhT_], [pb])
                    P.op("act", lambda e: e.copy(out=kt_[0:bs, :], in_=pb[0:bs, :]), reads=[pb], writes=[kt_])
                    yield
                    ats = []
                    for hg in range(0 if so else 2):
                        ps = PSF()
                        for hh in range(4):
                            h = hg * 4 + hh
                            mm(ps[0:bs, hh * 128:hh * 128 + bs], KT_[:, h, cols], QT_[:, h, cols], True, True, [KT_, QT_], [ps])
                        a_ = at[hg]
                        P.op("dve", lambda e: e.tensor_tensor(out=a_[0:bs, :, 0:bs], in0=ps[0:bs, :].rearrange("p (h t) -> p h t", h=4)[:, :, 0:bs],
                                                              in1=masks[0:bs, mi:mi + 1, 0:bs].broadcast_to([bs, 4, bs]), op=ALU.mult),
                             reads=[ps, masks], writes=[a_])
                        ats.append(a_)
                        yield
                    pos_ = pso
                    for c in range(cpb):
                        cg = b * cpb + c
                        if smp:
                            P.dma("sp", S[:], st_in[l, c].rearrange("h k v -> k h v"), writes=[S])
                        if not so:
                            P.op("dve", lambda e: e.tensor_tensor(out=Sb[:, 0:4, :], in0=S[:, 0:4, :], in1=er_[:, 0:4, cg:cg + 1].broadcast_to([128, 4, 128]), op=ALU.mult),
                                 reads=[S, er_], writes=[Sb])
                            P.op("pool", lambda e: e.tensor_tensor(out=Sb[:, 4:8, :], in0=S[:, 4:8, :], in1=er_[:, 4:8, cg:cg + 1].broadcast_to([128, 4, 128]), op=ALU.mult),
                                 reads=[S, er_], writes=[Sb])
                        if C >= 32:
                            rows = slice(c * C, (c + 1) * C)
                            ksrc = kt_
                            kap = lambda h: kt_[rows, h * 128:(h + 1) * 128]
                            vap = lambda h: vt_[rows, h * 128:(h + 1) * 128]
                        else:
                            km = ktm[0]
                            P.op("pool", lambda e: e.tensor_scalar(out=km[0:bs, :], in0=kt_[0:bs, :], scalar1=rowm[0:bs, c:c + 1], scalar2=None, op0=ALU.mult),
                                 reads=[kt_, rowm], writes=[km])
                            ksrc = km
                            kap = lambda h: km[0:bs, h * 128:(h + 1) * 128]
                            vap = lambda h: vt_[0:bs, h * 128:(h + 1) * 128]
                        pps = [PSF(), PSF()]
                        for h in range(8):
                            mm(pps[h // 4][:, (h % 4) * 128:(h % 4 + 1) * 128], kap(h), vap(h), True, True, [ksrc, vt_], [pps[h // 4]])
                        for h in range(0 if so else 8):
                            po = pos_[h // 4]
                            oc = slice((h % 4) * bs + c * C, (h % 4) * bs + (c + 1) * C)
                            mm(po[:, oc], vt_[0:bs, h * 128:(h + 1) * 128], ats[h // 4][0:bs, h % 4, c * C:(c + 1) * C], True, False, [vt_, ats[h // 4]], [po])
                            mm(po[:, oc], Sb[:, h, :], QT_[:, h, b * bs + c * C:b * bs + (c + 1) * C], False, True, [Sb, QT_], [po])
                        P.op("dve", lambda e: e.tensor_tensor(out=S[:], in0=S[:], in1=ebl_[:, :, cg:cg + 1].broadcast_to([128, 8, 128]), op=ALU.mult),
                             reads=[S, ebl_], writes=[S])
                        for hg in range(2):
                            P.op("dve", lambda e: e.tensor_tensor(out=S[:, hg * 4:(hg + 1) * 4, :], in0=pps[hg][:, :].rearrange("p (h v) -> p h v", h=4),
                                                                  in1=S[:, hg * 4:(hg + 1) * 4, :], op=ALU.add), reads=[pps[hg], S], writes=[S])
                        if smp:
                            P.dma("sp", sts[l, c].rearrange("h k v -> k h v"), S[:], reads=[S], writes=[B_out], sb=S)
                        yield
                    if so:
                        continue
                    for hg in range(2):
                        po = pos_[hg]
                        w = 4 * bs
                        P.op("act", lambda e: e.activation(out=sqo[:, 0:w], in_=po[:, 0:w], func=AF.Square), reads=[po], writes=[sqo])
                        ps = PSF()
                        mm(ps[:, 0:w], ones[:, :], sqo[:, 0:w], True, True, [ones, sqo], [ps])
                        P.op("act", lambda e: e.activation(out=rs[:, 0:w], in_=ps[:, 0:w], func=AF.Ln, scale=1.0 / 128, bias=EPS), reads=[ps], writes=[rs])
                        P.op("act", lambda e: e.activation(out=rs[:, 0:w], in_=rs[:, 0:w], func=AF.Exp, scale=-0.5), reads=[rs], writes=[rs])
                        P.op("dve", lambda e: e.tensor_tensor(out=rs[:, 0:w], in0=po[:, 0:w], in1=rs[:, 0:w], op=ALU.mult), reads=[po, rs], writes=[rs])
                        for hh in range(4):
                            h = hg * 4 + hh
                            P.op("dve", lambda e: e.scalar_tensor_tensor(out=ogT[:, h, cols], in0=rs[:, hh * bs:(hh + 1) * bs], scalar=ogain[:, h:h + 1],
                                                                        in1=sgo_[:, h, cols], op0=ALU.mult, op1=ALU.mult), reads=[rs, ogain, sgo_], writes=[ogT])
                        yield
                    for hf in range(2):
                        ps = PSF()
                        for h in range(8):
                            mm(ps[0:bs, :], ogT[:, h, cols], w_out[:, h, hf * 512:(hf + 1) * 512], h == 0, h == 7, [ogT, w_out], [ps])
                        P.op("dve", lambda e: e.tensor_tensor(out=xt[0:bs, b, hf * 512:(hf + 1) * 512], in0=ps[0:bs, :], in1=xt[0:bs, b, hf * 512:(hf + 1) * 512], op=ALU.add),
                             reads=[ps, xt], writes=[xt])
                        yield
                if not so:
                    P.dma("sp", dst.rearrange("(b p) f -> p b f", p=bs), xt[0:bs, 0:nb, :], reads=[xt], writes=wb, sb=xt)
                    if (not smp) and ti == len(tiles) - 2:
                        P.dma("sp", stp[l].rearrange("h k v -> k h v"), S[:], reads=[S], writes=[B_out], sb=S)
                doneB[ti] = True

            gens = []
            for ti in range(len(tiles)):
                gens.append(genA(ti))
                gens.append(genB(ti))
            pipeline(gens, depth=3 if so else 2, lag=1)
            if so:
                P.dma("sp", st_x[:, :].rearrange("(h k) v -> k h v", k=128), S[:], reads=[S], writes=[B_stx], sb=S)
        end_phase()
        ringbanks["l"] = psf + pso
        if so:
            P.collective(st_x.ap().opt(), st_g.ap().opt(), GROUPS, [B_stx], [B_stg])
            P.barrier()

    def hgrn_layer(l, src_p, src_s, dst_p, dst_s, first):
        with contextlib.ExitStack() as ost:
            vw = a_w_in[l].rearrange("(c p) f -> p c f", p=128)
            wp = {}
            order = [("f", 1), ("i", 2), ("og", 3), ("q", 0)]
            for kind, blk in order:
                for hlf in range(2):
                    t_ = sb("w_%s%d" % (kind, hlf), [128, 8, 512], BF16, ost)
                    c0 = blk * D + hlf * 512
                    P.dma("pool", t_[:], vw[:, :, c0:c0 + 512], writes=[t_])
                    wp[kind + str(hlf)] = t_
            hgrn_phase(l, src_p, src_s, dst_p, dst_s, first, state_only=True, wp=wp)
            hgrn_phase(l, src_p, src_s, dst_p, dst_s, first, wp=wp)

    def headnorm_rot(st_bufs, x3, nh, bs, gain4, cs, sn, ng, gbuf, cbuf):
        sq, ssq, tmp, X = st_bufs
        P.op("act", lambda e: e.activation(out=sq[0:bs, 0:nh, :], in_=x3, func=AF.Square), reads=[X], writes=[sq])
        yield
        P.op("dve", lambda e: e.tensor_reduce(out=ssq[0:bs, 0:nh], in_=sq[0:bs, 0:nh, :], axis=AX.X, op=ALU.add), reads=[sq], writes=[ssq])
        P.op("act", lambda e: e.activation(out=ssq[0:bs, 0:nh], in_=ssq[0:bs, 0:nh], func=AF.Ln, scale=1.0 / 128, bias=EPS), reads=[ssq], writes=[ssq])
        P.op("act", lambda e: e.activation(out=ssq[0:bs, 0:nh], in_=ssq[0:bs, 0:nh], func=AF.Exp, scale=-0.5), reads=[ssq], writes=[ssq])
        yield
        P.op("dve", lambda e: e.tensor_tensor(out=x3, in0=x3, in1=ssq[0:bs, 0:nh].unsqueeze(2).broadcast_to([bs, nh, 128]), op=ALU.mult), reads=[X, ssq], writes=[X])
        yield
        x4 = x3.rearrange("p (g h) d -> p g h d", g=ng)
        P.op("dve", lambda e: e.tensor_tensor(out=x4, in0=x4, in1=gain4, op=ALU.mult), reads=[X, gbuf], writes=[X])
        yield
        x1 = x3[:, :, 0:16]
        x2 = x3[:, :, 16:32]
        cb = cs.unsqueeze(1).broadcast_to([bs, nh, 16])
        sbb = sn.unsqueeze(1).broadcast_to([bs, nh, 16])
        t = [tmp[0:bs, i, 0:nh, :] for i in range(4)]
        P.op("dve", lambda e: e.tensor_tensor(out=t[0], in0=x1, in1=cb, op=ALU.mult), reads=[X, cbuf], writes=[tmp])
        P.op("dve", lambda e: e.tensor_tensor(out=t[1], in0=x2, in1=sbb, op=ALU.mult), reads=[X, cbuf], writes=[tmp])
        P.op("dve", lambda e: e.tensor_tensor(out=t[2], in0=x2, in1=cb, op=ALU.mult), reads=[X, cbuf], writes=[tmp])
        P.op("dve", lambda e: e.tensor_tensor(out=t[3], in0=x1, in1=sbb, op=ALU.mult), reads=[X, cbuf], writes=[tmp])
        yield
        P.op("dve", lambda e: e.tensor_tensor(out=x1, in0=t[0], in1=t[1], op=ALU.subtract), reads=[tmp], writes=[X])
        P.op("dve", lambda e: e.tensor_tensor(out=x2, in0=t[2], in1=t[3], op=ALU.add), reads=[tmp], writes=[X])
        yield

    def headnorm_rot_bf(st_bufs, x3, nh, bs, gain4b, cs, sn, ng, gbuf, cbuf, y3, Y):
        sq, ssq, tmp, X = st_bufs
        P.op("act", lambda e: e.activation(out=sq[0:bs, 0:nh, :], in_=x3, func=AF.Square), reads=[X], writes=[sq])
        yield
        P.op("dve", lambda e: e.tensor_reduce(out=ssq[0:bs, 0:nh], in_=sq[0:bs, 0:nh, :], axis=AX.X, op=ALU.add), reads=[sq], writes=[ssq])
        P.op("act", lambda e: e.activation(out=ssq[0:bs, 0:nh], in_=ssq[0:bs, 0:nh], func=AF.Ln, scale=1.0 / 128, bias=EPS), reads=[ssq], writes=[ssq])
        P.op("act", lambda e: e.activation(out=ssq[0:bs, 0:nh], in_=ssq[0:bs, 0:nh], func=AF.Exp, scale=-0.5), reads=[ssq], writes=[ssq])
        yield
        P.op("dve", lambda e: e.tensor_tensor(out=y3, in0=x3, in1=ssq[0:bs, 0:nh].unsqueeze(2).broadcast_to([bs, nh, 128]), op=ALU.mult), reads=[X, ssq], writes=[Y])
        yield
        y4 = y3.rearrange("p (g h) d -> p g h d", g=ng)
        P.op("dve", lambda e: e.tensor_tensor(out=y4, in0=y4, in1=gain4b, op=ALU.mult), reads=[Y, gbuf], writes=[Y])
        yield
        x1 = y3[:, :, 0:16]
        x2 = y3[:, :, 16:32]
        cb = cs.unsqueeze(1).broadcast_to([bs, nh, 16])
        sbb = sn.unsqueeze(1).broadcast_to([bs, nh, 16])
        t = [tmp[0:bs, i, 0:nh, :] for i in range(4)]
        P.op("dve", lambda e: e.tensor_tensor(out=t[0], in0=x1, in1=cb, op=ALU.mult), reads=[Y, cbuf], writes=[tmp])
        P.op("dve", lambda e: e.tensor_tensor(out=t[1], in0=x2, in1=sbb, op=ALU.mult), reads=[Y, cbuf], writes=[tmp])
        P.op("dve", lambda e: e.tensor_tensor(out=t[2], in0=x2, in1=cb, op=ALU.mult), reads=[Y, cbuf], writes=[tmp])
        P.op("dve", lambda e: e.tensor_tensor(out=t[3], in0=x1, in1=sbb, op=ALU.mult), reads=[Y, cbuf], writes=[tmp])
        yield
        P.op("dve", lambda e: e.tensor_tensor(out=x1, in0=t[0], in1=t[1], op=ALU.subtract), reads=[tmp], writes=[Y])
        P.op("dve", lambda e: e.tensor_tensor(out=x2, in0=t[2], in1=t[3], op=ALU.add), reads=[tmp], writes=[Y])
        yield

    def kv_phase():
        TT = 256
        with contextlib.ExitStack() as st:
            w = sb("w_kvs", [128, 8, 1536], BF16, st)
            load_w(w, w_kv, 8, 1536, 4)
            N = NormCtx(st, TT, kv_norm, nxt=3)
            kg = sb("kg", [128, 3, 128], F32, st)
            P.dma("sp", kg[:], k_norm.partition_broadcast(128), writes=[kg])
            csL = [sb("cs%d" % i, [128, 2, 2, 16], F32, st) for i in range(3)]
            kcL = [sb("kc%d" % i, [128, 6, 128], F32, st) for i in range(2)]
            vvL = [sb("vv%d" % i, [128, 3, 256], F32, st) for i in range(2)]
            k16L = [sb("kvh16%d" % i, [128, 3, 512], BF16, st) for i in range(2)]
            sqL = [sb("hsq%d" % i, [128, 6, 128], F32, st) for i in range(2)]
            ssqL = [sb("hssq%d" % i, [128, 6], F32, st) for i in range(2)]
            tmpL = [sb("htmp%d" % i, [128, 4, 6, 16], F32, st) for i in range(2)]
            tctx = {}

            TL = tile_list(TT)

            lctx = {}

            def preload(ti):
                smp, s0, n = TL[ti]
                bs = min(128, n)
                nb = n // bs
                src = (xss if smp else xs[s0:s0 + n, :])
                xt = N.load(src, n, xbufs(smp, s0, n))
                cs = csL[ti % 3]
                cc, ss_ = (c_cos_s, c_sin_s) if smp else (c_cos[s0:s0 + n, :], c_sin[s0:s0 + n, :])
                P.dma("act", cs[0:bs, 0:nb, 0, :], cc.rearrange("(b p) f -> p b f", p=bs), writes=[cs])
                P.dma("act", cs[0:bs, 0:nb, 1, :], ss_.rearrange("(b p) f -> p b f", p=bs), writes=[cs])
                lctx[ti] = (xt, cs)

            def prep(ti):
                smp, s0, n = TL[ti]
                if ti not in lctx:
                    preload(ti)
                xt, cs = lctx[ti]
                tctx[ti] = (N.norm(xt, n), cs)

            def blk_gen(ti, smp, s0, n, b):
                bs = min(128, n)
                nb = n // bs
                if ti == 0 and b == 0:
                    prep(0)
                if b == 0 and ti + 1 < len(TL):
                    preload(ti + 1)
                if b == nb - 1 and ti + 1 < len(TL):
                    prep(ti + 1)
                    yield
                xnT, cs = tctx[ti]
                kc, vv, kvh16 = kcL[b & 1], vvL[b & 1], k16L[b & 1]
                cols = slice(b * bs, (b + 1) * bs)
                for g in range(3):
                    ps = PSF()
                    for kc_ in range(8):
                        mm(ps[0:bs, :], xnT[:, kc_, cols], w[:, kc_, g * 512:(g + 1) * 512], kc_ == 0, kc_ == 7, [w, xnT], [ps])
                    P.op("act", lambda e: e.copy(out=kc[0:bs, 2 * g:2 * g + 2, :], in_=ps[0:bs, 0:256].rearrange("p (h d) -> p h d", h=2)), reads=[ps], writes=[kc])
                    P.op("act", lambda e: e.copy(out=vv[0:bs, g, :], in_=ps[0:bs, 256:512]), reads=[ps], writes=[vv])
                    yield
                for _ in headnorm_rot((sqL[b & 1], ssqL[b & 1], tmpL[b & 1], kc), kc[0:bs, :, :], 6, bs,
                                      kg[0:bs, :, :].unsqueeze(2).broadcast_to([bs, 3, 2, 128]), cs[0:bs, b, 0, :], cs[0:bs, b, 1, :], 3, kg, cs):
                    yield
                if smp:
                    dk = kvs.rearrange("p (g c) -> p g c", g=3)
                    P.dma("sp", dk[:, :, 0:256], kc[0:bs, :, :].rearrange("p (g h) d -> p g (h d)", g=3), reads=[kc], writes=[B_kvs], sb=kc)
                    P.dma("sp", dk[:, :, 256:512], vv[0:bs, :, :], reads=[vv], writes=[B_kvs], sb=vv)
                else:
                    r0 = s0 + b * bs
                    dk = kvp[r0:r0 + bs, :].rearrange("p (g c) -> p g c", g=3)
                    dkb = kvb16[r0:r0 + bs, :].rearrange("p (g c) -> p g c", g=3)
                    P.dma("sp", dk[:, :, 0:256], kc[0:bs, :, :].rearrange("p (g h) d -> p g (h d)", g=3), reads=[kc], writes=[B_kvp], sb=kc)
                    P.dma("sp", dk[:, :, 256:512], vv[0:bs, :, :], reads=[vv], writes=[B_kvp], sb=vv)
                    P.op("act", lambda e: e.copy(out=kvh16[0:bs, :, 0:256], in_=kc[0:bs, :, :].rearrange("p (g h) d -> p g (h d)", g=3)), reads=[kc], writes=[kvh16])
                    P.op("pool", lambda e: e.tensor_copy(out=kvh16[0:bs, :, 256:512], in_=vv[0:bs, :, :]), reads=[vv], writes=[kvh16])
                    P.dma("sp", dkb[:, :, :], kvh16[0:bs, :, :], reads=[kvh16], writes=[B_kvb], sb=kvh16)

            gens = []
            for ti, (smp, s0, n) in enumerate(tile_list(TT)):
                for b in range(max(1, n // 128)):
                    gens.append(blk_gen(ti, smp, s0, n, b))
            pipeline(gens, depth=2, lag=7)
        end_phase()

    def kv_exchange():
        for i, (g_, o_, n_) in enumerate(WX):
            r0 = T - WINS[g_] + o_
            P.dma("pool", wx_t[i][:, :], kvp[r0:r0 + n_, g_ * 512:(g_ + 1) * 512], reads=[B_kvp], writes=[B_wx[i]], sb=B_wx[i])
        for i, (g_, o_, n_) in enumerate(WX):
            P.collective(wx_t[i].ap().opt(), wg_t[i].ap().opt(), GROUPS, [B_wx[i]], [B_kvg])
        for i, (g_, o_, n_) in enumerate(WX):
            P.dma("pool", wgb[g_][o_:o_ + n_, :], wg_t[i][0:n_, :], reads=[B_kvg], writes=[B_wgb], sb=B_wgb, chain=False)
        for g in range(3):
            P.dma("pool", winp[g], kvp[T - WINS[g]:T, g * 512:(g + 1) * 512], reads=[B_kvp], writes=[B_out], sb=B_out, chain=False)
        for g in range(3):
            flat = cache[g].rearrange("s t f -> (s t) f")
            rows = NSQ * WINS[g]
            step = min(rows, 1024)
            for r0 in range(0, rows, step):
                P.dma("pool", cb16[g][r0:r0 + step, :], flat[r0:r0 + step, :], writes=[B_cb], sb=B_cb, chain=False)
        P.dma("pool", kvs16, kvs, reads=[B_kvs], writes=[B_cb], sb=B_cb, chain=False)

    def q_phase(j):
        TT = 256
        with contextlib.ExitStack() as st:
            wq = [sb("w_q%d" % i, [128, 8, 512], BF16, st) for i in range(6)]
            vq = b_w_q[j].rearrange("(c p) f -> p c f", p=128)
            for i in range(6):
                P.dma("pool", wq[i][:], vq[:, :, i * 512:(i + 1) * 512], writes=[wq[i]])
            if j == 0:
                kv_exchange()
            N = NormCtx(st, TT, b_norm[j], nxt=3)
            qg = sb("qg", [128, 3, 128], F32, st)
            P.dma("sp", qg[:], q_norm[j].partition_broadcast(128), writes=[qg])
            qg16 = sb("qg16", [128, 3, 128], BF16, st)
            P.op("dve", lambda e: e.tensor_copy(out=qg16[:], in_=qg[:]), reads=[qg], writes=[qg16])
            csL = [sb("cs%d" % i, [128, 2, 2, 16], F32, st) for i in range(3)]
            qt = [sb("qt%d" % i, [128, 24, 128], F32, st) for i in range(2)]
            qb = [sb("qb%d" % i, [128, 3072], BF16, st) for i in range(2)]
            sqL = [sb("hsq%d" % i, [128, 24, 128], BF16, st) for i in range(2)]
            ssqL = [sb("hssq%d" % i, [128, 24], F32, st) for i in range(2)]
            tmpL = [sb("htmp%d" % i, [128, 4, 24, 16], F32, st) for i in range(2)]
            tctx = {}

            TL = tile_list(TT)

            lctx = {}

            def preload(ti):
                smp, s0, n = TL[ti]
                bs = min(128, n)
                nb = n // bs
                src = (xss if smp else xs[s0:s0 + n, :])
                xt = N.load(src, n, xbufs(smp, s0, n))
                cs = csL[ti % 3]
                cc, ss_ = (c_cos_s, c_sin_s) if smp else (c_cos[s0:s0 + n, :], c_sin[s0:s0 + n, :])
                P.dma("act", cs[0:bs, 0:nb, 0, :], cc.rearrange("(b p) f -> p b f", p=bs), writes=[cs])
                P.dma("act", cs[0:bs, 0:nb, 1, :], ss_.rearrange("(b p) f -> p b f", p=bs), writes=[cs])
                lctx[ti] = (xt, cs)

            def prep(ti):
                smp, s0, n = TL[ti]
                if ti not in lctx:
                    preload(ti)
                xt, cs = lctx[ti]
                tctx[ti] = (N.norm(xt, n), cs)

            def blk_gen(ti, smp, s0, n, b, gi):
                bs = min(128, n)
                nb = n // bs
                if ti == 0 and b == 0:
                    prep(0)
                if b == 0 and ti + 1 < len(TL):
                    preload(ti + 1)
                if b == nb - 1 and ti + 1 < len(TL):
                    prep(ti + 1)
                    yield
                xnT, cs = tctx[ti]
                cols = slice(b * bs, (b + 1) * bs)
                q_ = qt[gi & 1]
                qb_ = qb[gi & 1]
                for c6 in range(6):
                    ps = PSF()
                    for kc_ in range(8):
                        mm(ps[0:bs, :], xnT[:, kc_, cols], wq[c6][:, kc_, :], kc_ == 0, kc_ == 7, [wq[c6], xnT], [ps])
                    P.op("act", lambda e: e.copy(out=q_[0:bs, c6 * 4:(c6 + 1) * 4, :], in_=ps[0:bs, :].rearrange("p (h d) -> p h d", h=4)), reads=[ps], writes=[q_])
                    yield
                for _ in headnorm_rot_bf((sqL[gi & 1], ssqL[gi & 1], tmpL[gi & 1], q_), q_[0:bs, :, :], 24, bs,
                                         qg16[0:bs, :, :].unsqueeze(2).broadcast_to([bs, 3, 8, 128]), cs[0:bs, b, 0, :], cs[0:bs, b, 1, :], 3, qg16, cs,
                                         qb_[0:bs, :].rearrange("p (h d) -> p h d", h=24), qb_):
                    yield
                if smp:
                    P.dma("sp", qs, qb_[0:bs, :], reads=[qb_], writes=[B_qs], sb=qb_)
                else:
                    P.dma("sp", qp[s0 + b * bs:s0 + (b + 1) * bs, :], qb_[0:bs, :], reads=[qb_], writes=[B_qp], sb=qb_)

            gens = []
            for ti, (smp, s0, n) in enumerate(tile_list(TT)):
                for b in range(max(1, n // 128)):
                    gens.append(blk_gen(ti, smp, s0, n, b, len(gens)))
            pipeline(gens, depth=2, lag=7)
        end_phase()

    def att_phase(j, dst_p, dst_s, last):
        with contextlib.ExitStack() as st:
            w_o = sb("w_o", [128, 8, D], BF16, st)
            load_w(w_o, b_w_o[j], 8, D, 2)
            numT = sb("numT", [128, 4, 2048], F32, st)
            denT = sb("denT", [128, 4, 2048], F32, st)
            oT = sb("oT", [128, 8, 2048], BF16, st)
            qb = [sb("aqb%d" % i, [128, 512], BF16, st) for i in range(8)]
            QTb = [sb("aQT%d" % i, [128, 512], BF16, st) for i in range(8)]
            kvb = [sb("akv%d" % i, [128, 2, 128], BF16, st) for i in range(16)]
            KTb = [sb("aKT%d" % i, [128, 128], BF16, st) for i in range(16)]
            Eb = [sb("aE%d" % i, [128, 512], BF16, st) for i in range(8)]
            Pb = [sb("aP%d" % i, [128, 512], BF16, st) for i in range(8)]
            xt = [sb("axt%d" % i, [128, D], F32, st) for i in range(2)]
            xo = [sb("axo%d" % i, [128, D], F32, st) for i in range(2)]
            cnt = {"q": 0, "k": 0, "e": 0, "x": 0}
            mbias = sb("mbias", [128, 3, 512], BF16, st)
            P.dma("pool", mbias[:], c_mbias, writes=[mbias])

            def load_kv_dma(g, kvh, tok0, dil, src_rows=None):
                cnt["k"] += 1
                kb, kt_ = kvb[cnt["k"] % 16], KTb[cnt["k"] % 16]
                if src_rows is None:
                    if tok0 >= 0:
                        b5 = kvb16.rearrange("t (g c h d) -> t g c h d", g=3, c=2, h=2)
                        P.dma("sp", kb[:], b5[ss(tok0, 128, dil), g, :, kvh, :], reads=[B_kvb], writes=[kb])
                    else:
                        r_ = tok0 + 128 * dil
                        w4 = wgb[g].rearrange("t (c h d) -> t c h d", c=2, h=2)
                        P.dma("sp", kb[:], w4[ss(r_, 128, dil), :, kvh, :], reads=[B_wgb], writes=[kb])
                else:
                    for i_, (p0, np_, ap_) in enumerate(src_rows):
                        P.dma("sp", kb[p0:p0 + np_, :, :], ap_, reads=[B_cb], writes=[kb], chain=(i_ == 0))
                return kb, kt_

            def kv_transpose(kb, kt_):
                pb = PSB()
                tr(pb[:, 0:128], kb[:, 0, :], 128, [kb], [pb])
                P.op("act", lambda e: e.copy(out=kt_[:], in_=pb[:, 0:128]), reads=[pb], writes=[kt_])

            def load_kv(g, kvh, tok0, dil, src_rows=None):
                kb, kt_ = load_kv_dma(g, kvh, tok0, dil, src_rows)
                kv_transpose(kb, kt_)
                return kb, kt_

            own_of = {}

            def blk_gen(kvh, g, r, bl):
                dil = DILS[g]
                tok0 = dil * 128 * bl + r
                cnt["q"] += 1
                q_, QT_ = qb[cnt["q"] % 8], QTb[cnt["q"] % 8]
                c0 = g * 1024 + kvh * 512
                P.dma("sp", q_[:], qp[ss(tok0, 128, dil), c0:c0 + 512], reads=[B_qp], writes=[q_])
                pv_new = load_kv_dma(g, kvh, tok0 - 128 * dil, dil) if bl == 0 else None
                own = load_kv_dma(g, kvh, tok0, dil)
                own_of[(kvh, g, r, bl)] = own
                for _ in range(3):
                    yield
                pb = PSB()
                for hh in range(4):
                    tr(pb[:, hh * 128:(hh + 1) * 128], q_[:, hh * 128:(hh + 1) * 128], 128, [q_], [pb])
                P.op("act", lambda e: e.copy(out=QT_[:], in_=pb[:, 0:512]), reads=[pb], writes=[QT_])
                if pv_new is not None:
                    kv_transpose(*pv_new)
                    prev = pv_new
                else:
                    prev = own_of[(kvh, g, r, bl - 1)]
                kv_transpose(*own)
                yield
                parts = [(own, 2), (prev, 4 if bl == 0 else 1)]
                Ps = []
                for (kb, kt_), mi in parts:
                    cnt["e"] += 1
                    E_, P_ = Eb[cnt["e"] % 8], Pb[cnt["e"] % 8]
                    ps = PSF()
                    mm(ps[:, :], kt_[:, :], QT_[:, :], True, False, [kt_, QT_], [ps])
                    mm(ps[:, :], ident[:, :], mbias[:, {2: 0, 1: 1, 4: 2}[mi], :], False, True, [ident, mbias], [ps])
                    P.op("act", lambda e: e.activation(out=E_[:], in_=ps[:, :], func=AF.Exp, scale=SCALE), reads=[ps], writes=[E_])
                    Ps.append((kb, E_))
                    yield
                po = PSF()
                for i, (kb, P_) in enumerate(Ps):
                    mm(po[:, :], kb[:, 1, :], P_[:, :], i == 0, i == len(Ps) - 1, [kb, P_], [po])
                pd = PSF()
                for i, (kb, P_) in enumerate(Ps):
                    mm(pd[:, :], ones[:, :], P_[:, :], i == 0, i == len(Ps) - 1, [ones, P_], [pd])
                nv = numT[:, :, ss(tok0, 128, dil)]
                dv = denT[:, :, ss(tok0, 128, dil)]
                po3 = po[:, :].rearrange("p (h t) -> p h t", h=4)
                pd3 = pd[:, :].rearrange("p (h t) -> p h t", h=4)
                if g == 0:
                    P.op("act", lambda e: e.copy(out=nv, in_=po3), reads=[po], writes=[numT])
                    P.op("dve", lambda e: e.tensor_copy(out=dv, in_=pd3), reads=[pd], writes=[denT])
                else:
                    P.op("dve", lambda e: e.tensor_tensor(out=nv, in0=po3, in1=nv, op=ALU.add), reads=[po, numT], writes=[numT])
                    P.op("dve", lambda e: e.tensor_tensor(out=dv, in0=pd3, in1=dv, op=ALU.add), reads=[pd, denT], writes=[denT])

            for kvh in range(2):
                gens = [blk_gen(kvh, g, r, bl) for g in range(3) for r in range(DILS[g]) for bl in range(16 // DILS[g])]
                pipeline(gens, depth=6, lag=1)
                for hh in range(4):
                    P.op("act", lambda e: e.activation(out=denT[:, hh, :], in_=denT[:, hh, :], func=AF.Ln), reads=[denT], writes=[denT])
                    P.op("act", lambda e: e.activation(out=denT[:, hh, :], in_=denT[:, hh, :], func=AF.Exp, scale=-1.0), reads=[denT], writes=[denT])
                    P.op("pool", lambda e: e.tensor_tensor(out=oT[:, kvh * 4 + hh, :], in0=numT[:, hh, :], in1=denT[:, hh, :], op=ALU.mult), reads=[numT, denT], writes=[oT])

            def wo_gen(tb):
                cnt["x"] += 1
                x_, o_ = xt[cnt["x"] & 1], xo[cnt["x"] & 1]
                P.dma("sp", x_[:], xs[tb * 128:(tb + 1) * 128, :], reads=[B_xs[tb]], writes=[x_])
                yield
                for hf in range(2):
                    ps = PSF()
                    for h in range(8):
                        mm(ps[:, :], oT[:, h, tb * 128:(tb + 1) * 128], w_o[:, h, hf * 512:(hf + 1) * 512], h == 0, h == 7, [oT, w_o], [ps])
                    P.op("dve", lambda e: e.tensor_tensor(out=o_[:, hf * 512:(hf + 1) * 512], in0=ps[:, :], in1=x_[:, hf * 512:(hf + 1) * 512], op=ALU.add), reads=[ps, x_], writes=[o_])
                    yield
                P.dma("sp", dst_p[tb * 128:(tb + 1) * 128, :], o_[:], reads=[o_], writes=[B_xs[tb]], sb=o_)

            pipeline([wo_gen(tb) for tb in range(16)], depth=2, lag=1)

            qsb = sb("qsb", [16, 3072], BF16, st)
            QTs = sb("QTs", [128, 24, 16], BF16, st)
            ksf = sb("ksf", [16, 3, 2, 2, 128], BF16, st)
            KsT = sb("KsT", [128, 6, 16], BF16, st)
            numS = sb("numS", [128, 2, 4, 16], F32, st)
            denS = sb("denS", [128, 2, 4, 16], F32, st)
            oTs = sb("oTs", [128, 8, 16], BF16, st)
            Es = [sb("Es%d" % i, [128, 64], BF16, st) for i in range(2)]
            P.dma("sp", qsb[:], qs, reads=[B_qs], writes=[qsb])
            P.dma("pool", ksf[:], kvs.rearrange("t (g c h d) -> t g c h d", g=3, c=2, h=2), reads=[B_kvs], writes=[ksf])
            pb = PSB()
            for gh in range(24):
                tr(pb[:, gh * 16:(gh + 1) * 16], qsb[0:16, gh * 128:(gh + 1) * 128], 16, [qsb], [pb])
            P.op("act", lambda e: e.copy(out=QTs[:, :, :], in_=pb[:, 0:384].rearrange("p (a t) -> p a t", a=24)), reads=[pb], writes=[QTs])
            pb = PSB()
            for g in range(3):
                for kvh in range(2):
                    tr(pb[:, (g * 2 + kvh) * 16:(g * 2 + kvh + 1) * 16], ksf[0:16, g, 0, kvh, :], 16, [ksf], [pb])
            P.op("act", lambda e: e.copy(out=KsT[:, :, :], in_=pb[:, 0:96].rearrange("p (a t) -> p a t", a=6)), reads=[pb], writes=[KsT])
            ec = 0
            for kvh in range(2):
                for g in range(3):
                    dil = DILS[g]
                    ps = PSF()
                    for hh in range(4):
                        mm(ps[0:16, hh * 16:(hh + 1) * 16], KsT[:, g * 2 + kvh, :], QTs[:, g * 8 + kvh * 4 + hh, :], True, True, [KsT, QTs], [ps])
                    ec += 1
                    E_ = Es[ec & 1]
                    P.op("act", lambda e: e.activation(out=E_[0:16, 0:64], in_=ps[0:16, 0:64], func=AF.Exp, scale=SCALE), reads=[ps], writes=[E_])
                    P.op("dve", lambda e: e.tensor_tensor(out=E_[0:16, 0:64].rearrange("p (h t) -> p h t", h=4), in0=E_[0:16, 0:64].rearrange("p (h t) -> p h t", h=4),
                                                          in1=ident[0:16, 0:16].unsqueeze(1).broadcast_to([16, 4, 16]), op=ALU.mult), reads=[E_, ident], writes=[E_])
                    po = PSF()
                    mm(po[:, 0:64], ksf[0:16, g, 1, kvh, :], E_[0:16, 0:64], True, True, [ksf, E_], [po])
                    pd = PSF()
                    mm(pd[:, 0:64], ones[0:16, :], E_[0:16, 0:64], True, True, [ones, E_], [pd])
                    nvs = numS[:, kvh, :, :].rearrange("p h t -> p (h t)")
                    dvs = denS[:, kvh, :, :].rearrange("p h t -> p (h t)")
                    if g == 0:
                        P.op("act", lambda e: e.copy(out=nvs, in_=po[:, 0:64]), reads=[po], writes=[numS])
                        P.op("dve", lambda e: e.tensor_copy(out=dvs, in_=pd[:, 0:64]), reads=[pd], writes=[denS])
                    else:
                        P.op("dve", lambda e: e.tensor_tensor(out=nvs, in0=po[:, 0:64], in1=nvs, op=ALU.add), reads=[po, numS], writes=[numS])
                        P.op("dve", lambda e: e.tensor_tensor(out=dvs, in0=pd[:, 0:64], in1=dvs, op=ALU.add), reads=[pd, denS], writes=[denS])
                    c5 = cb16[g].rearrange("(s t) (c h d) -> s t c h d", s=NSQ, c=2, h=2)
                    k5 = kvs16.rearrange("t (g c h d) -> t g c h d", g=3, c=2, h=2)

                    def unit_gen(kvh, g, dil, sq_, t, c5, k5):
                        tok = sq_ * 4 + t
                        if g == 0 and t > 0:
                            rows = [(0, 128 - t, c5[sq_, t:128, :, kvh, :]), (128 - t, t, k5[sq_ * 4:sq_ * 4 + t, g, :, kvh, :])]
                        else:
                            rows = [(0, 128, c5[sq_, ss(t, 128, dil), :, kvh, :])]
                        kb, kt_ = load_kv_dma(g, kvh, 0, dil, src_rows=rows)
                        for _ in range(4):
                            yield
                        kv_transpose(kb, kt_)
                        yield
                        ps = PSF()
                        q4 = QTs[:, g * 8 + kvh * 4:g * 8 + kvh * 4 + 4, tok]
                        mm(ps[:, 0:4], kt_[:, :], q4, True, True, [kt_, QTs], [ps])
                        cnt["e"] += 1
                        E_ = Eb[cnt["e"] % 8]
                        P.op("act", lambda e: e.activation(out=E_[:, 0:4], in_=ps[:, 0:4], func=AF.Exp, scale=SCALE), reads=[ps], writes=[E_])
                        yield
                        po = PSF()
                        mm(po[:, 0:4], kb[:, 1, :], E_[:, 0:4], True, True, [kb, E_], [po])
                        pd = PSF()
                        mm(pd[:, 0:4], ones[:, :], E_[:, 0:4], True, True, [ones, E_], [pd])
                        P.op("dve", lambda e: e.tensor_tensor(out=numS[:, kvh, :, tok], in0=po[:, 0:4], in1=numS[:, kvh, :, tok], op=ALU.add), reads=[po, numS], writes=[numS])
                        P.op("dve", lambda e: e.tensor_tensor(out=denS[:, kvh, :, tok], in0=pd[:, 0:4], in1=denS[:, kvh, :, tok], op=ALU.add), reads=[pd, denS], writes=[denS])

                    pipeline([unit_gen(kvh, g, dil, sq_, t, c5, k5) for sq_ in range(NSQ) for t in range(4)], depth=8, lag=1)
            P.op("act", lambda e: e.activation(out=denS[:], in_=denS[:], func=AF.Ln), reads=[denS], writes=[denS])
            P.op("act", lambda e: e.activation(out=denS[:], in_=denS[:], func=AF.Exp, scale=-1.0), reads=[denS], writes=[denS])
            P.op("dve", lambda e: e.tensor_tensor(out=oTs[:, :, :], in0=numS[:, :, :, :].rearrange("p k h t -> p (k h) t"), in1=denS[:, :, :, :].rearrange("p k h t -> p (k h) t"), op=ALU.mult),
                 reads=[numS, denS], writes=[oTs])
            x_, o_ = xt[0], xo[0]
            P.dma("sp", x_[0:16, :], xss, reads=[B_xss], writes=[x_])
            for hf in range(2):
                ps = PSF()
                for h in range(8):
                    mm(ps[0:16, :], oTs[:, h, :], w_o[:, h, hf * 512:(hf + 1) * 512], h == 0, h == 7, [oTs, w_o], [ps])
                P.op("dve", lambda e: e.tensor_tensor(out=o_[0:16, hf * 512:(hf + 1) * 512], in0=ps[0:16, :], in1=x_[0:16, hf * 512:(hf + 1) * 512], op=ALU.add), reads=[ps, x_], writes=[o_])
            P.dma("sp", dst_s, o_[0:16, :], reads=[o_], writes=[B_xss], sb=o_)
        end_phase()

    def full():
        hgrn_layer(0, xp, xsm, xs, xss, True)
        mlp_phase(0, xs, xss, xs, xss, False, False)
        hgrn_layer(1, xs, xss, xs, xss, False)
        mlp_phase(1, xs, xss, xs, xss, False, False)
        kv_phase()
        q_phase(0)
        att_phase(0, xs, xss, False)
        mlp_phase(2, xs, xss, xs, xss, False, False)
        q_phase(1)
        att_phase(1, xs, xss, False)
        mlp_phase(3, xs, xss, yp, ysm, False, True)
        P.barrier()

    if mode == "full":
        full()
        return nc, P
    if mode == "att":
        for i in range(T // 128):
            P.dma("sp", xs[i * 128:(i + 1) * 128, :], xp[i * 128:(i + 1) * 128, :], writes=[B_xs[i]], sb=B_xs[i])
        P.dma("sp", xss, xsm, writes=[B_xss], sb=B_xss)
        P.barrier()
        kv_phase()
        q_phase(0)
        att_phase(0, yp, ysm, False)
        P.barrier()
        return nc, P
    if mode == "hgrn":
        hgrn_phase(0, xp, xsm, yp, ysm, True)
        P.barrier()
        return nc, P
    if mode == "mlp":
        mlp_phase(0, xp, xsm, yp, ysm, True, True)
        P.barrier()
        return nc, P
    return nc, P


def _consts(hp=0):
    half = 16
    inv = (np.float32(THETA) ** (-2.0 * np.arange(half, dtype=np.float32) / np.float32(32))).astype(np.float32)
    pos = np.arange(T, dtype=np.float32) + np.float32(hp * T)
    ang = (pos[:, None] * inv[None, :]).astype(np.float32)
    pos_s = np.tile(np.arange(4, dtype=np.float32) + np.float32(PAST), NSQ)
    ang_s = (pos_s[:, None] * inv[None, :]).astype(np.float32)
    u = np.arange(128)[:, None]
    t = np.arange(128)[None, :]
    m = np.zeros((128, 5, 128), np.float32)
    m[:, 0] = ((u // 64) == (t // 64)) & (t >= u)
    m[:, 1] = (u >= t)
    m[:, 2] = (u <= t)
    m[:, 3] = ((u // 4) == (t // 4)) & (t >= u)
    m[:, 4] = m[:, 1] * float(hp)
    sc = np.ones((2, 2048), np.float32)
    sc[0, ::64] = 0
    sc[1, ::4] = 0
    return {
        "c_cos": np.cos(ang).astype(np.float32), "c_sin": np.sin(ang).astype(np.float32),
        "c_cos_s": np.cos(ang_s).astype(np.float32), "c_sin_s": np.sin(ang_s).astype(np.float32),
        "c_ident": np.eye(128, dtype=np.float32), "c_masks": m, "c_scan": sc,
        "c_rowm": (np.arange(128)[:, None] // 4 == np.arange(4)[None, :]).astype(np.float32),
        "c_flag": np.full((128, 1), float(hp), np.float32),
        "c_mbias": np.stack([np.tile(np.where(m[:, k] > 0, 0.0, -30000.0), (1, 4)) for k in (2, 1, 4)], axis=1).astype(np.float32),
    }


_CACHE = {}


def kernel(**inputs):
    inp = {k: np.ascontiguousarray(np.asarray(v)) for k, v in inputs.items()}
    if "nc" not in _CACHE:
        _CACHE["nc"] = build("full")[0]
    nc = _CACHE["nc"]
    consts = [_consts(0), _consts(1)]
    wnames = ["a_norm", "a_w_in", "a_lb_logits", "a_out_norm", "a_w_out", "kv_norm", "w_kv", "k_norm",
              "b_norm", "b_w_q", "q_norm", "b_w_o", "mlp_norm", "mlp_w_up", "mlp_w_down"]
    in_maps = []
    for c in range(8):
        m = {k: inp[k] for k in wnames}
        m.update(consts[c % 2])
        m["xp"] = inp["x_prompt"][c // 2, (c % 2) * T:(c % 2 + 1) * T]
        m["xsm"] = inp["x_sample"][4 * c:4 * c + 4].reshape(NST, D)
        m["st_in"] = np.ascontiguousarray(inp["state_hgrn"][:, 4 * c:4 * c + 4])
        m["c1"] = inp["cache_win1_kv"][4 * c:4 * c + 4].reshape(NSQ, 128, 512)
        m["c2"] = inp["cache_win2_kv"][4 * c:4 * c + 4].reshape(NSQ, 512, 512)
        m["c3"] = inp["cache_win3_kv"][4 * c:4 * c + 4].reshape(NSQ, 2048, 512)
        in_maps.append(m)
    res = run_bass_kernel_spmd(nc, in_maps, core_ids=list(range(8))).results
    y_prompt = np.stack([np.concatenate([res[2 * i]["yp"], res[2 * i + 1]["yp"]], axis=0) for i in range(4)], axis=0)
    y_sample = np.concatenate([res[c]["ysm"].reshape(NSQ, 4, D) for c in range(8)], axis=0)
    st_p = np.stack([res[2 * i + 1]["stp"] for i in range(4)], axis=1)
    st_s = np.concatenate([res[c]["sts"] for c in range(8)], axis=1)
    wins_p = [np.stack([res[c]["w%dp" % (g + 1)].reshape(WINS[g], 2, 2, 128) for c in (1, 3, 5, 7)], axis=0) for g in range(3)]
    wins_s = [np.concatenate([res[c]["kvs"][:, g * 512:(g + 1) * 512].reshape(NSQ, 4, 2, 2, 128) for c in range(8)], axis=0) for g in range(3)]
    f = lambda a: np.ascontiguousarray(a, dtype=np.float32)
    return (f(y_prompt), f(y_sample), f(st_p), f(st_s), f(wins_p[0]), f(wins_p[1]), f(wins_p[2]),
            f(wins_s[0]), f(wins_s[1]), f(wins_s[2]))
```

```python
import contextlib
import numpy as np
import ml_dtypes
import concourse.bass as bass
import concourse.mybir as mybir
from concourse.bass_utils import run_bass_kernel_spmd

F32 = mybir.dt.float32
BF16 = mybir.dt.bfloat16
AF = mybir.ActivationFunctionType
ALU = mybir.AluOpType
AX = mybir.AxisListType

T = 2048
D = 1024
NSQ = 4
NST = 16
EPS = 1e-6
PAST = 8192
THETA = 500000.0
WINS = (128, 512, 2048)
DILS = (1, 4, 16)
SCALE = 128 ** -0.5


def ss(start, n, step):
    return slice(start, start + (n - 1) * step + 1, step)


def pipeline(gens, depth=2, lag=1):
    active = []
    it = iter(gens)
    pending = next(it, None)
    while active or pending is not None:
        if pending is not None and len(active) < depth and (not active or active[-1][1] >= lag):
            active.append([pending, 0])
            pending = next(it, None)
        for a in list(active):
            try:
                next(a[0])
                a[1] += 1
            except StopIteration:
                active.remove(a)


class Buf:
    __slots__ = ("t", "w", "r", "dsem", "dcnt", "name")

    def __init__(self, t, name=""):
        self.t = t
        self.w = None
        self.r = {}
        self.dsem = None
        self.dcnt = 0
        self.name = name

    def __getitem__(self, k):
        return self.t[k]


class Prog:
    def __init__(self, nc):
        self.nc = nc
        self.eng = {"pe": nc.tensor, "act": nc.scalar, "dve": nc.vector, "pool": nc.gpsimd, "sp": nc.sync}
        self.sem = {e: nc.alloc_semaphore("s_" + e) for e in ("pe", "act", "dve", "pool")}
        self.cnt = {e: 0 for e in self.sem}
        self.seen = {e: {} for e in self.eng}
        self.semobj = {}
        self.dma_bufs = []
        self.n_ins = 0
        self.free_sems = []
        self.nsem = 0

    def _key(self, sem):
        k = id(sem)
        self.semobj[k] = sem
        return k

    def _deps(self, reads, writes):
        need = {}
        for b in reads:
            if b.w is not None:
                k, v = b.w
                need[k] = max(need.get(k, 0), v)
        for b in writes:
            if b.w is not None:
                k, v = b.w
                need[k] = max(need.get(k, 0), v)
            for k, v in b.r.items():
                need[k] = max(need.get(k, 0), v)
        return need

    def _wait(self, e, need):
        own = self._key(self.sem["pe"]) if e == "pe" else None
        for k, v in need.items():
            if k == own:
                continue
            if self.seen[e].get(k, 0) >= v:
                continue
            self.eng[e].wait_ge(self.semobj[k], v)
            self.seen[e][k] = v

    def _mark(self, ev, reads, writes):
        k, v = ev
        for b in reads:
            b.r[k] = max(b.r.get(k, 0), v)
        for b in writes:
            b.w = ev
            b.r = {}

    def op(self, e, fn, reads=(), writes=()):
        self._wait(e, self._deps(reads, writes))
        ins = fn(self.eng[e])
        self.cnt[e] += 1
        ins.then_inc(self.sem[e], 1)
        self._mark((self._key(self.sem[e]), self.cnt[e]), reads, writes)
        self.n_ins += 1

    def dma(self, q, out, in_, reads=(), writes=(), sb=None, slow=False, chain=True):
        if sb is None:
            sb = writes[0]
        if sb.dsem is None:
            if self.free_sems:
                sb.dsem, sb.dcnt = self.free_sems.pop()
            else:
                self.nsem += 1
                sb.dsem = self.nc.alloc_semaphore("d%d" % self.nsem)
            self.dma_bufs.append(sb)
        need = self._deps(reads, writes)
        k = self._key(sb.dsem)
        if sb.dcnt and chain:
            need[k] = max(need.get(k, 0), sb.dcnt)
        self._wait(q, need)
        ins = self.eng[q].dma_start(out=out, in_=in_, allow_slow_non_contiguous=slow)
        sb.dcnt += 16
        ins.then_inc(sb.dsem, 16)
        self._mark((k, sb.dcnt), reads, writes)
        self.n_ins += 1

    def collective(self, ins_ap, outs_ap, groups, reads, writes):
        if not hasattr(self, "ccsem"):
            self.ccsem = self.nc.alloc_semaphore("ccsem")
            self.cccnt = 0
            self.ccbuf = Buf(None, "cc")
            self.ccbuf.dsem = self.ccsem
            self.dma_bufs.append(self.ccbuf)
        need = self._deps(reads, writes)
        k = self._key(self.ccsem)
        if self.cccnt:
            need[k] = self.cccnt
        self._wait("pool", need)
        ins = self.eng["pool"].collective_compute("AllGather", ALU.bypass, replica_groups=groups, ins=[ins_ap], outs=[outs_ap])
        self.cccnt += 1
        self.ccbuf.dcnt = self.cccnt
        ins.then_inc(self.ccsem)
        self._mark((k, self.cccnt), reads, writes)

    def release(self, bufs):
        for b in bufs:
            if b.dsem is not None and b in self.dma_bufs:
                self.dma_bufs.remove(b)
                self.free_sems.append((b.dsem, b.dcnt))
                b.dsem = None

    def barrier(self):
        need = {}
        for e in self.sem:
            if self.cnt[e]:
                need[self._key(self.sem[e])] = self.cnt[e]
        for b in self.dma_bufs:
            if b.dcnt:
                need[self._key(b.dsem)] = b.dcnt
        for e in self.eng:
            self._wait(e, dict(need))


def build(mode="full"):
    nc = bass.Bass("TRN2", target_bir_lowering=False)
    P = Prog(nc)

    def din(name, shape, dt=F32):
        return nc.dram_tensor(name, list(shape), dt, kind="ExternalInput").ap()

    def dout(name, shape, dt=F32):
        return nc.dram_tensor(name, list(shape), dt, kind="ExternalOutput").ap()

    def dscr(name, shape, dt=F32):
        return nc.dram_tensor(name, list(shape), dt, kind="Internal").ap()

    xp = din("xp", [T, D])
    xsm = din("xsm", [NST, D])
    st_in = din("st_in", [2, NSQ, 8, 128, 128])
    cache = [din("c1", [NSQ, 128, 512]), din("c2", [NSQ, 512, 512]), din("c3", [NSQ, 2048, 512])]
    a_norm = din("a_norm", [2, D])
    a_w_in = din("a_w_in", [2, D, 4 * D])
    a_lb = din("a_lb_logits", [2, D])
    a_out_norm = din("a_out_norm", [2, D])
    a_w_out = din("a_w_out", [2, D, D])
    kv_norm = din("kv_norm", [D])
    w_kv = din("w_kv", [D, 1536])
    k_norm = din("k_norm", [3, 128])
    b_norm = din("b_norm", [2, D])
    b_w_q = din("b_w_q", [2, D, 3072])
    q_norm = din("q_norm", [2, 3, 128])
    b_w_o = din("b_w_o", [2, D, D])
    mlp_norm = din("mlp_norm", [4, D])
    mlp_w_up = din("mlp_w_up", [4, D, 4 * D])
    mlp_w_down = din("mlp_w_down", [4, 4 * D, D])
    c_cos = din("c_cos", [T, 16])
    c_sin = din("c_sin", [T, 16])
    c_cos_s = din("c_cos_s", [NST, 16])
    c_sin_s = din("c_sin_s", [NST, 16])
    c_ident = din("c_ident", [128, 128])
    c_masks = din("c_masks", [128, 5, 128])
    c_scan = din("c_scan", [2, 2048])
    c_rowm = din("c_rowm", [128, 4])
    c_mbias = din("c_mbias", [128, 3, 512])
    c_flag = din("c_flag", [128, 1])

    yp = dout("yp", [T, D])
    ysm = dout("ysm", [NST, D])
    stp = dout("stp", [2, 8, 128, 128])
    sts = dout("sts", [2, NSQ, 8, 128, 128])
    winp = [dout("w1p", [128, 512]), dout("w2p", [512, 512]), dout("w3p", [2048, 512])]
    kvs = dout("kvs", [NST, 1536])

    xs = dscr("xs", [T, D])
    xss = dscr("xss", [NST, D])
    qp = dscr("qp", [T, 3072], BF16)
    qs = dscr("qs", [NST, 3072], BF16)
    st_x = nc.dram_tensor("st_x", [1024, 128], F32)
    st_g = nc.dram_tensor("st_g", [2048, 128], F32)
    kvp_t = nc.dram_tensor("kvp", [T, 1536], F32)
    kvb16 = dscr("kvb16", [T, 1536], BF16)
    B_kvb = Buf(None, "kvb16")
    WX = [(0, 0, 128), (1, 0, 512), (2, 0, 1024), (2, 1024, 1024)]
    wx_t = [nc.dram_tensor("wx%d" % i, [n_, 512], F32) for i, (g_, o_, n_) in enumerate(WX)]
    wg_t = [nc.dram_tensor("wg%d" % i, [2 * n_, 512], F32) for i, (g_, o_, n_) in enumerate(WX)]
    B_wx = [Buf(None, "wx%d" % i) for i in range(4)]
    wgb = [dscr("wgb%d" % g, [WINS[g], 512], BF16) for g in range(3)]
    cb16 = [dscr("cb16_%d" % g, [NSQ * WINS[g], 512], BF16) for g in range(3)]
    kvs16 = dscr("kvs16", [NST, 1536], BF16)
    B_cb = Buf(None, "cb16")
    B_wgb = Buf(None, "wgb")
    B_stx = Buf(None, "stx")
    B_stg = Buf(None, "stg")
    B_kvg = Buf(None, "kvg")
    GROUPS = [[0, 1], [2, 3], [4, 5], [6, 7]]

    B_xs = [Buf(None, "xs%d" % i) for i in range(T // 128)]
    B_xss = Buf(None, "xss")
    B_kvp = Buf(None, "kvp")
    B_kvs = Buf(None, "kvs")
    B_qp = Buf(None, "qp")
    B_qs = Buf(None, "qs")
    B_out = Buf(None, "out")
    kvp = kvp_t.ap()

    stack = contextlib.ExitStack()

    uniq = [0]

    phase_bufs = []

    def sb(name, shape, dt=F32, st=None):
        uniq[0] += 1
        t = (st or stack).enter_context(nc.sbuf_tensor("%s_%d" % (name, uniq[0]), list(shape), dt))
        b = Buf(t, name)
        if st is not None:
            phase_bufs.append(b)
        return b

    def end_phase():
        P.barrier()
        P.release(phase_bufs)
        del phase_bufs[:]

    ident = sb("ident", [128, 128], BF16)
    masks = sb("masks", [128, 5, 128], BF16)
    scanm = sb("scanm", [128, 2, 256], BF16)
    ones = sb("ones", [128, 128], BF16)
    P.dma("pool", ident[:], c_ident, writes=[ident])
    P.dma("pool", masks[:], c_masks, writes=[masks])
    P.dma("pool", scanm[:], c_scan[:, 0:256].partition_broadcast(128), writes=[scanm])
    P.op("dve", lambda e: e.memset(ones[:], 1.0), writes=[ones])

    psf = []
    for i in range(4):
        psf.append(Buf(stack.enter_context(nc.psum_tensor("psf%d" % i, [128, 512], F32)), "psf%d" % i))
    pso = [Buf(stack.enter_context(nc.psum_tensor("pso%d" % i, [128, 512], F32)), "pso%d" % i) for i in range(2)]
    psb = []
    for i in range(2):
        psb.append(Buf(stack.enter_context(nc.psum_tensor("psb%d" % i, [128, 1024], BF16)), "psb%d" % i))
    ring = {"f": 0, "b": 0}

    def PSF():
        ring["f"] = (ring["f"] + 1) % len(psf)
        return psf[ring["f"]]

    def PSB():
        ring["b"] = (ring["b"] + 1) % len(psb)
        return psb[ring["b"]]

    def mm(out_ap, lhsT, rhs, start, stop, reads, writes):
        P.op("pe", lambda e: e.matmul(out_ap, lhsT=lhsT, rhs=rhs, start=start, stop=stop), reads=reads, writes=writes)

    def tr(out_ap, in_ap, n, reads, writes):
        P.op("pe", lambda e: e.transpose(out_ap, in_ap, ident[0:n, 0:n]), reads=list(reads) + [ident], writes=writes)

    def load_w(dst, src_ap, kc, fdim, nsplit):
        v = src_ap.rearrange("(c p) f -> p c f", p=128)
        step = kc // nsplit
        for i in range(nsplit):
            P.dma("pool", dst[:, i * step:(i + 1) * step, :], v[:, i * step:(i + 1) * step, :], writes=[dst], chain=(i == 0))

    def rstd_from_ssq(out_ap, in_ap, n, bufs_r, bufs_w):
        P.op("act", lambda e: e.activation(out=out_ap, in_=in_ap, func=AF.Ln, scale=1.0 / n, bias=EPS), reads=bufs_r, writes=bufs_w)
        P.op("act", lambda e: e.activation(out=out_ap, in_=out_ap, func=AF.Exp, scale=-0.5), reads=bufs_w, writes=bufs_w)

    class NormCtx:
        def __init__(self, st, tt, gain_ap, nxt=2, nT=2):
            self.tt = tt
            nb = max(1, tt // 128)
            self.gain = sb("gain", [128, D], F32, st)
            P.dma("sp", self.gain[:], gain_ap.partition_broadcast(128), writes=[self.gain])
            self.xt = [sb("xt%d" % i, [128, nb, D], F32, st) for i in range(nxt)]
            self.xn = [sb("xn%d" % i, [128, D], BF16, st) for i in range(2)]
            self.sq = sb("sqj", [128, D], BF16, st)
            self.ssq = [sb("ssq%d" % i, [128, 2], F32, st) for i in range(2)]
            self.xnT = [sb("xnT%d" % i, [128, 8, tt], BF16, st) for i in range(nT)]
            self.i = 0
            self.li = 0

        def load(self, src_ap, n, src_bufs):
            self.li = (self.li + 1) % len(self.xt)
            xt = self.xt[self.li]
            bs = min(128, n)
            nb = n // bs
            P.dma("act", xt[0:bs, 0:nb, :], src_ap.rearrange("(b p) f -> p b f", p=bs), reads=src_bufs, writes=[xt])
            return xt

        def norm(self, xt, n):
            self.i = (self.i + 1) % len(self.xnT)
            xnT = self.xnT[self.i]
            bs = min(128, n)
            nb = n // bs
            for b in range(nb):
                ssq = self.ssq[b & 1]
                xn = self.xn[b & 1]
                P.op("act", lambda e: e.activation(out=self.sq[0:bs, :], in_=xt[0:bs, b, :], func=AF.Square, accum_out=ssq[0:bs, 0:1]),
                     reads=[xt], writes=[self.sq, ssq])
                rstd_from_ssq(ssq[0:bs, 1:2], ssq[0:bs, 0:1], D, [ssq], [ssq])
                P.op("dve", lambda e: e.scalar_tensor_tensor(out=xn[0:bs, :], in0=xt[0:bs, b, :], scalar=ssq[0:bs, 1:2], in1=self.gain[0:bs, :],
                                                            op0=ALU.mult, op1=ALU.mult), reads=[xt, ssq, self.gain], writes=[xn])
                pb = PSB()
                for c in range(8):
                    tr(pb[:, c * 128:c * 128 + bs], xn[0:bs, c * 128:(c + 1) * 128], bs, [xn], [pb])
                P.op("act", lambda e: e.copy(out=xnT[:, :, b * bs:(b + 1) * bs], in_=pb[:, :].rearrange("p (c t) -> p c t", c=8)[:, :, 0:bs]),
                     reads=[pb], writes=[xnT])
            return xnT

        def run(self, src_ap, n, src_bufs):
            xt = self.load(src_ap, n, src_bufs)
            return xt, self.norm(xt, n)

    def tile_list(tt):
        return [(False, i * tt, tt) for i in range(T // tt)] + [(True, 0, NST)]

    def xbufs(smp, s0, n):
        return [B_xss] if smp else B_xs[s0 // 128:(s0 + n) // 128]

    def mlp_phase(l, src_p, src_s, dst_p, dst_s, first, last):
        TT = 256
        with contextlib.ExitStack() as st:
            w_up = [sb("w_up%d" % i, [128, 8, 512], BF16, st) for i in range(8)]
            w_dn = [sb("w_dn%d" % i, [128, 4, D], BF16, st) for i in range(8)]
            vu = mlp_w_up[l].rearrange("(c p) f -> p c f", p=128)
            vd = mlp_w_down[l].rearrange("(c p) f -> p c f", p=128)
            for i in range(8):
                P.dma("pool", w_up[i][:], vu[:, :, i * 512:(i + 1) * 512], writes=[w_up[i]])
            for i in range(8):
                P.dma("pool", w_dn[i][:], vd[:, i * 4:(i + 1) * 4, :], writes=[w_dn[i]])
            N = NormCtx(st, TT, mlp_norm[l])
            hT = [sb("hT%d" % i, [128, 32, TT], BF16, st) for i in range(2)]
            rl = [sb("rl%d" % i, [128, TT], F32, st) for i in range(2)]

            def tile_gen(ti, smp, s0, n):
                src = (src_s if smp else src_p[s0:s0 + n, :])
                dst = (dst_s if smp else dst_p[s0:s0 + n, :])
                rb = [] if first else xbufs(smp, s0, n)
                wb = [B_out] if last else xbufs(smp, s0, n)
                xt, xnT = N.run(src, n, rb)
                yield
                h = hT[ti & 1]
                bs = min(128, n)
                nb = n // bs
                for fc in range(32):
                    ps = PSF()
                    for kc in range(8):
                        mm(ps[:, 0:n], w_up[fc // 4][:, kc, (fc % 4) * 128:(fc % 4 + 1) * 128], xnT[:, kc, 0:n], kc == 0, kc == 7, [w_up[fc // 4], xnT], [ps])
                    r = rl[fc & 1]
                    P.op("act", lambda e: e.activation(out=r[:, 0:n], in_=ps[:, 0:n], func=AF.Relu), reads=[ps], writes=[r])
                    P.op("dve", lambda e: e.tensor_tensor(out=h[:, fc, 0:n], in0=r[:, 0:n], in1=r[:, 0:n], op=ALU.mult),
                         reads=[r], writes=[h])
                    if fc & 1:
                        yield
                for b in range(nb):
                    for hf in range(2):
                        ps = PSF()
                        for fc in range(32):
                            mm(ps[0:bs, :], h[:, fc, b * bs:(b + 1) * bs], w_dn[fc // 4][:, fc % 4, hf * 512:(hf + 1) * 512], fc == 0, fc == 31, [h, w_dn[fc // 4]], [ps])
                        P.op("dve", lambda e: e.tensor_tensor(out=xt[0:bs, b, hf * 512:(hf + 1) * 512], in0=ps[0:bs, :], in1=xt[0:bs, b, hf * 512:(hf + 1) * 512], op=ALU.add),
                             reads=[ps, xt], writes=[xt])
                        yield
                P.dma("sp", dst.rearrange("(b p) f -> p b f", p=bs), xt[0:bs, 0:nb, :], reads=[xt], writes=wb, sb=xt)

            pipeline((tile_gen(ti, *t) for ti, t in enumerate(tile_list(TT))), depth=2, lag=10)
        end_phase()

    rowm = sb("rowm", [128, 4], F32)
    P.dma("sp", rowm[:], c_rowm, writes=[rowm])
    flag = sb("flag", [128, 1], F32)
    P.dma("sp", flag[:], c_flag, writes=[flag])

    def hgrn_phase(l, src_p, src_s, dst_p, dst_s, first, state_only=False, wp=None):
        TT = 256
        so = state_only
        with contextlib.ExitStack() as st:
            def wcol(kind, h_):
                p_ = wp[kind + str(h_ // 4)]
                return p_, p_[:, :, (h_ % 4) * 128:(h_ % 4 + 1) * 128]

            if so:
                w_out = None
            else:
                w_out = sb("w_out", [128, 8, D], BF16, st)
                load_w(w_out, a_w_out[l], 8, D, 2)
            N = NormCtx(st, TT, a_norm[l], nxt=3 if so else 2, nT=3 if so else 2)
            oml = sb("oml", [128, 8], F32, st)
            lbt = sb("lbt", [128, 2, 8], F32, st)
            ogain = sb("ogain", [128, 8], F32, st)
            P.dma("sp", lbt[:], a_lb.rearrange("l (h k) -> k l h", k=128), writes=[lbt], slow=True)
            P.dma("sp", ogain[:], a_out_norm[l].rearrange("(h k) -> k h", k=128), writes=[ogain], slow=True)
            if l == 0:
                P.op("dve", lambda e: e.memset(oml[:], 1.0), writes=[oml])
            else:
                P.op("dve", lambda e: e.tensor_tensor(out=oml[:], in0=lbt[:, 0, :], in1=lbt[:, 1, :], op=ALU.subtract), reads=[lbt], writes=[oml])
                P.op("act", lambda e: e.activation(out=oml[:], in_=oml[:], func=AF.Sigmoid), reads=[oml], writes=[oml])
            NB2 = 3 if so else 2
            NA = 2 if so else 1
            qf = None if so else sb("qf", [128, 8, TT], BF16, st)
            kfL = [sb("kf%d" % i, [128, 8, TT], F32, st) for i in range(NA)]
            glL = [sb("gl%d" % i, [128, 8, TT], F32, st) for i in range(NA)]
            eeL = [sb("ee%d" % i, [128, 8, TT], F32, st) for i in range(NA)]
            refcL = [sb("refc%d" % i, [128, 8, 4], F32, st) for i in range(NA)]
            QT = [None] if so else [sb("QT%d" % i, [128, 8, TT], BF16, st) for i in range(2)]
            KT = [sb("KT%d" % i, [128, 8, TT], BF16, st) for i in range(NB2)]
            KhT = [sb("KhT%d" % i, [128, 8, TT], BF16, st) for i in range(NB2)]
            sgo = [None] if so else [sb("sgo%d" % i, [128, 8, TT], BF16, st) for i in range(2)]
            er = [sb("er%d" % i, [128, 8, 4], F32, st) for i in range(NB2)]
            ebr = [sb("ebr%d" % i, [128, 8, 4], F32, st) for i in range(NB2)]
            ebl = [sb("ebl%d" % i, [128, 8, 4], F32, st) for i in range(NB2)]
            vt = [sb("vt%d" % i, [128, D], BF16, st) for i in range(2)]
            kt = [sb("kt%d" % i, [128, D], BF16, st) for i in range(2)]
            ktm = [sb("ktm%d" % i, [128, D], BF16, st) for i in range(1)]
            at = [None, None] if so else [sb("at%d" % i, [128, 4, 128], BF16, st) for i in range(2)]
            S = sb("S", [128, 8, 128], F32, st)
            Sb = None if so else sb("Sb", [128, 8, 128], BF16, st)
            sqo = None if so else sb("sqo", [128, 512], BF16, st)
            rs = None if so else sb("rs", [128, 512], F32, st)
            ogT = None if so else sb("ogT", [128, 8, TT], BF16, st)
            if so:
                P.op("dve", lambda e: e.memset(S[:], 0.0), writes=[S])
                tiles = tile_list(TT)[:-1]
            else:
                P.dma("sp", S[:], st_g[0:1024, :].rearrange("(h k) v -> k h v", k=128), reads=[B_stg], writes=[S])
                P.op("dve", lambda e: e.tensor_scalar(out=S[:].rearrange("p h v -> p (h v)"), in0=S[:].rearrange("p h v -> p (h v)"), scalar1=flag[:, 0:1], scalar2=None, op0=ALU.mult),
                     reads=[S, flag], writes=[S])
                tiles = tile_list(TT)
            bic = [0]
            doneA, doneB, tctx = {}, {}, {}

            def tparams(ti):
                smp, s0, n = tiles[ti]
                C = 4 if smp else 64
                bs = min(128, n)
                p2 = ti % NB2
                pq = 0 if so else (ti & 1)
                return dict(smp=smp, s0=s0, n=n, C=C, bs=bs, nb=n // bs, cpb=bs // C, nch=n // C, mi=3 if smp else 0,
                            QT_=QT[pq], KT_=KT[p2], KhT_=KhT[p2], sgo_=sgo[pq], er_=er[p2], ebr_=ebr[p2], ebl_=ebl[p2])

            def genA(ti):
                tp = tparams(ti)
                smp, s0, n, C, nch = tp["smp"], tp["s0"], tp["n"], tp["C"], tp["nch"]
                QT_, KT_, KhT_, sgo_, er_, ebr_, ebl_ = tp["QT_"], tp["KT_"], tp["KhT_"], tp["sgo_"], tp["er_"], tp["ebr_"], tp["ebl_"]
                kf, gl, ee, refc = kfL[ti % NA], glL[ti % NA], eeL[ti % NA], refcL[ti % NA]
                src = (src_s if smp else src_p[s0:s0 + n, :])
                rb = [] if first else xbufs(smp, s0, n)
                xt, xnT = N.run(src, n, rb)
                tctx[ti] = (xt, xnT)
                yield
                for h in range(8):
                    ps = PSF()
                    for kc in range(8):
                        mm(ps[:, 0:n], wcol('f', h)[1][:, kc, :], xnT[:, kc, 0:n], kc == 0, kc == 7, [wcol('f', h)[0], xnT], [ps])
                    P.op("act", lambda e: e.activation(out=kf[:, h, 0:n], in_=ps[:, 0:n], func=AF.Sigmoid, scale=-1.0), reads=[ps], writes=[kf])
                    yield
                kv_ = kf[:, :, 0:n]
                gv_ = gl[:, :, 0:n]
                ev_ = ee[:, :, 0:n]
                P.op("dve", lambda e: e.tensor_tensor(out=kv_, in0=kv_, in1=oml[:, :].unsqueeze(2).broadcast_to([128, 8, n]), op=ALU.mult), reads=[kf, oml], writes=[kf])
                if not so:
                    for h in range(8):
                        ps = PSF()
                        for kc in range(8):
                            mm(ps[:, 0:n], wcol('og', h)[1][:, kc, :], xnT[:, kc, 0:n], kc == 0, kc == 7, [wcol('og', h)[0], xnT], [ps])
                        P.op("act", lambda e: e.activation(out=sgo_[:, h, 0:n], in_=ps[:, 0:n], func=AF.Sigmoid), reads=[ps], writes=[sgo_])
                        yield
                P.op("act", lambda e: e.activation(out=gv_, in_=kv_, func=AF.Ln, scale=-1.0, bias=1.0), reads=[kf], writes=[gl])
                yield

                def qproj(h):
                    if so:
                        return
                    ps = PSF()
                    for kc in range(8):
                        mm(ps[:, 0:n], wcol('q', h)[1][:, kc, :], xnT[:, kc, 0:n], kc == 0, kc == 7, [wcol('q', h)[0], xnT], [ps])
                    P.op("act", lambda e: e.activation(out=qf[:, h, 0:n], in_=ps[:, 0:n], func=AF.Silu), reads=[ps], writes=[qf])

                for h in range(8):
                    P.op("dve", lambda e: e.tensor_tensor_scan(out=gl[:, h, 0:n], data0=scanm[:, 1 if smp else 0, 0:n], data1=gl[:, h, 0:n], initial=0.0,
                                                              op0=ALU.mult, op1=ALU.add), reads=[gl, scanm], writes=[gl])
                    qproj(h)
                    yield
                g4 = gv_.rearrange("p h (c t) -> p h c t", t=C)
                P.op("dve", lambda e: e.tensor_copy(out=refc[:, :, 0:nch], in_=g4[:, :, :, C // 2 - 1:C // 2].rearrange("p h c o -> p h (c o)")), reads=[gl], writes=[refc])
                P.op("dve", lambda e: e.tensor_tensor(out=g4, in0=g4, in1=refc[:, :, 0:nch].unsqueeze(3).broadcast_to([128, 8, nch, C]), op=ALU.subtract),
                     reads=[gl, refc], writes=[gl])
                yield
                P.op("act", lambda e: e.activation(out=er_[:, :, 0:nch], in_=refc[:, :, 0:nch], func=AF.Exp), reads=[refc], writes=[er_])
                P.op("act", lambda e: e.activation(out=ev_, in_=gv_, func=AF.Exp), reads=[gl], writes=[ee])
                yield
                if not so:
                    P.op("pool", lambda e: e.tensor_tensor(out=QT_[:, :, 0:n], in0=qf[:, :, 0:n], in1=ev_, op=ALU.mult), reads=[qf, ee], writes=[QT_])
                e4 = ev_.rearrange("p h (c t) -> p h c t", t=C)
                P.op("dve", lambda e: e.tensor_copy(out=ebr_[:, :, 0:nch], in_=e4[:, :, :, C - 1:C].rearrange("p h c o -> p h (c o)")), reads=[ee], writes=[ebr_])
                yield
                P.op("act", lambda e: e.activation(out=ev_, in_=gv_, func=AF.Exp, scale=-1.0), reads=[gl], writes=[ee])
                P.op("dve", lambda e: e.tensor_tensor(out=KT_[:, :, 0:n], in0=kv_, in1=ev_, op=ALU.mult), reads=[kf, ee], writes=[KT_])
                yield
                P.op("pool", lambda e: e.tensor_tensor(out=KhT_[:, :, 0:n].rearrange("p h (c t) -> p h c t", t=C), in0=KT_[:, :, 0:n].rearrange("p h (c t) -> p h c t", t=C),
                                                       in1=ebr_[:, :, 0:nch].unsqueeze(3).broadcast_to([128, 8, nch, C]), op=ALU.mult), reads=[KT_, ebr_], writes=[KhT_])
                P.op("dve", lambda e: e.tensor_tensor(out=ebl_[:, :, 0:nch], in0=er_[:, :, 0:nch], in1=ebr_[:, :, 0:nch], op=ALU.mult), reads=[er_, ebr_], writes=[ebl_])
                doneA[ti] = True

            def genB(ti):
                while not doneA.get(ti):
                    yield
                while ti > 0 and not doneB.get(ti - 1):
                    yield
                tp = tparams(ti)
                smp, s0, n, C, bs, nb, cpb, mi = tp["smp"], tp["s0"], tp["n"], tp["C"], tp["bs"], tp["nb"], tp["cpb"], tp["mi"]
                QT_, KT_, KhT_, sgo_, er_, ebr_, ebl_ = tp["QT_"], tp["KT_"], tp["KhT_"], tp["sgo_"], tp["er_"], tp["ebr_"], tp["ebl_"]
                xt, xnT = tctx[ti]
                dst = (dst_s if smp else dst_p[s0:s0 + n, :])
                wb = xbufs(smp, s0, n)
                for b in range(nb):
                    bic[0] += 1
                    vt_, kt_ = vt[bic[0] & 1], kt[bic[0] & 1]
                    cols = slice(b * bs, (b + 1) * bs)
                    for hf in range(2):
                        ps = PSF()
                        for kc in range(8):
                            mm(ps[0:bs, :], xnT[:, kc, cols], wp['i' + str(hf)][:, kc, :], kc == 0, kc == 7, [wp['i' + str(hf)], xnT], [ps])
                        P.op("act", lambda e: e.copy(out=vt_[0:bs, hf * 512:(hf + 1) * 512], in_=ps[0:bs, :]), reads=[ps], writes=[vt_])
                        yield
                    pb = PSB()
                    for h in range(8):
                        tr(pb[0:bs, h * 128:(h + 1) * 128], KhT_[:, h, cols], 128, [KhT_], [pb])
                    P.op("act", lambda e: e.copy(out=kt_[0:bs, :], in_=pb[0:bs, :]), reads=[pb], writes=[kt_])
                    yield
                    ats = []
                    for hg in range(0 if so else 2):
                        ps = PSF()
                        for hh in range(4):
                            h = hg * 4 + hh
                            mm(ps[0:bs, hh * 128:hh * 128 + bs], KT_[:, h, cols], QT_[:, h, cols], True, True, [KT_, QT_], [ps])
                        a_ = at[hg]
                        P.op("dve", lambda e: e.tensor_tensor(out=a_[0:bs, :, 0:bs], in0=ps[0:bs, :].rearrange("p (h t) -> p h t", h=4)[:, :, 0:bs],
                                                              in1=masks[0:bs, mi:mi + 1, 0:bs].broadcast_to([bs, 4, bs]), op=ALU.mult),
                             reads=[ps, masks], writes=[a_])
                        ats.append(a_)
                        yield
                    pos_ = pso
                    for c in range(cpb):
                        cg = b * cpb + c
                        if smp:
                            P.dma("sp", S[:], st_in[l, c].rearrange("h k v -> k h v"), writes=[S])
                        if not so:
                            P.op("dve", lambda e: e.tensor_tensor(out=Sb[:, 0:4, :], in0=S[:, 0:4, :], in1=er_[:, 0:4, cg:cg + 1].broadcast_to([128, 4, 128]), op=ALU.mult),
                                 reads=[S, er_], writes=[Sb])
                            P.op("pool", lambda e: e.tensor_tensor(out=Sb[:, 4:8, :], in0=S[:, 4:8, :], in1=er_[:, 4:8, cg:cg + 1].broadcast_to([128, 4, 128]), op=ALU.mult),
                                 reads=[S, er_], writes=[Sb])
                        if C >= 32:
                            rows = slice(c * C, (c + 1) * C)
                            ksrc = kt_
                            kap = lambda h: kt_[rows, h * 128:(h + 1) * 128]
                            vap = lambda h: vt_[rows, h * 128:(h + 1) * 128]
                        else:
                            km = ktm[0]
                            P.op("pool", lambda e: e.tensor_scalar(out=km[0:bs, :], in0=kt_[0:bs, :], scalar1=rowm[0:bs, c:c + 1], scalar2=None, op0=ALU.mult),
                                 reads=[kt_, rowm], writes=[km])
                            ksrc = km
                            kap = lambda h: km[0:bs, h * 128:(h + 1) * 128]
                            vap = lambda h: vt_[0:bs, h * 128:(h + 1) * 128]
                        pps = [PSF(), PSF()]
                        for h in range(8):
                            mm(pps[h // 4][:, (h % 4) * 128:(h % 4 + 1) * 128], kap(h), vap(h), True, True, [ksrc, vt_], [pps[h // 4]])
                        for h in range(0 if so else 8):
                            po = pos_[h // 4]
                            oc = slice((h % 4) * bs + c * C, (h % 4) * bs + (c + 1) * C)
                            mm(po[:, oc], vt_[0:bs, h * 128:(h + 1) * 128], ats[h // 4][0:bs, h % 4, c * C:(c + 1) * C], True, False, [vt_, ats[h // 4]], [po])
                            mm(po[:, oc], Sb[:, h, :], QT_[:, h, b * bs + c * C:b * bs + (c + 1) * C], False, True, [Sb, QT_], [po])
                        P.op("dve", lambda e: e.tensor_tensor(out=S[:], in0=S[:], in1=ebl_[:, :, cg:cg + 1].broadcast_to([128, 8, 128]), op=ALU.mult),
                             reads=[S, ebl_], writes=[S])
                        for hg in range(2):
                            P.op("dve", lambda e: e.tensor_tensor(out=S[:, hg * 4:(hg + 1) * 4, :], in0=pps[hg][:, :].rearrange("p (h v) -> p h v", h=4),
                                                                  in1=S[:, hg * 4:(hg + 1) * 4, :], op=ALU.add), reads=[pps[hg], S], writes=[S])
                        if smp:
                            P.dma("sp", sts[l, c].rearrange("h k v -> k h v"), S[:], reads=[S], writes=[B_out], sb=S)
                        yield
                    if so:
                        continue
                    for hg in range(2):
                        po = pos_[hg]
                        w = 4 * bs
                        P.op("act", lambda e: e.activation(out=sqo[:, 0:w], in_=po[:, 0:w], func=AF.Square), reads=[po], writes=[sqo])
                        ps = PSF()
                        mm(ps[:, 0:w], ones[:, :], sqo[:, 0:w], True, True, [ones, sqo], [ps])
                        P.op("act", lambda e: e.activation(out=rs[:, 0:w], in_=ps[:, 0:w], func=AF.Ln, scale=1.0 / 128, bias=EPS), reads=[ps], writes=[rs])
                        P.op("act", lambda e: e.activation(out=rs[:, 0:w], in_=rs[:, 0:w], func=AF.Exp, scale=-0.5), reads=[rs], writes=[rs])
                        P.op("dve", lambda e: e.tensor_tensor(out=rs[:, 0:w], in0=po[:, 0:w], in1=rs[:, 0:w], op=ALU.mult), reads=[po, rs], writes=[rs])
                        for hh in range(4):
                            h = hg * 4 + hh
                            P.op("dve", lambda e: e.scalar_tensor_tensor(out=ogT[:, h, cols], in0=rs[:, hh * bs:(hh + 1) * bs], scalar=ogain[:, h:h + 1],
                                                                        in1=sgo_[:, h, cols], op0=ALU.mult, op1=ALU.mult), reads=[rs, ogain, sgo_], writes=[ogT])
                        yield
                    for hf in range(2):
                        ps = PSF()
                        for h in range(8):
                            mm(ps[0:bs, :], ogT[:, h, cols], w_out[:, h, hf * 512:(hf + 1) * 512], h == 0, h == 7, [ogT, w_out], [ps])
                        P.op("dve", lambda e: e.tensor_tensor(out=xt[0:bs, b, hf * 512:(hf + 1) * 512], in0=ps[0:bs, :], in1=xt[0:bs, b, hf * 512:(hf + 1) * 512], op=ALU.add),
                             reads=[ps, xt], writes=[xt])
                        yield
                if not so:
                    P.dma("sp", dst.rearrange("(b p) f -> p b f", p=bs), xt[0:bs, 0:nb, :], reads=[xt], writes=wb, sb=xt)
                    if (not smp) and ti == len(tiles) - 2:
                        P.dma("sp", stp[l].rearrange("h k v -> k h v"), S[:], reads=[S], writes=[B_out], sb=S)
                doneB[ti] = True

            gens = []
            for ti in range(len(tiles)):
                gens.append(genA(ti))
                gens.append(genB(ti))
            pipeline(gens, depth=3 if so else 2, lag=1)
            if so:
                P.dma("sp", st_x[:, :].rearrange("(h k) v -> k h v", k=128), S[:], reads=[S], writes=[B_stx], sb=S)
        end_phase()
        if so:
            P.collective(st_x.ap().opt(), st_g.ap().opt(), GROUPS, [B_stx], [B_stg])
            P.barrier()

    def hgrn_layer(l, src_p, src_s, dst_p, dst_s, first):
        with contextlib.ExitStack() as ost:
            vw = a_w_in[l].rearrange("(c p) f -> p c f", p=128)
            wp = {}
            order = [("f", 1), ("i", 2), ("og", 3), ("q", 0)]
            for kind, blk in order:
                for hlf in range(2):
                    t_ = sb("w_%s%d" % (kind, hlf), [128, 8, 512], BF16, ost)
                    c0 = blk * D + hlf * 512
                    P.dma("pool", t_[:], vw[:, :, c0:c0 + 512], writes=[t_])
                    wp[kind + str(hlf)] = t_
            hgrn_phase(l, src_p, src_s, dst_p, dst_s, first, state_only=True, wp=wp)
            hgrn_phase(l, src_p, src_s, dst_p, dst_s, first, wp=wp)

    def headnorm_rot(st_bufs, x3, nh, bs, gain4, cs, sn, ng, gbuf, cbuf):
        sq, ssq, tmp, X = st_bufs
        P.op("act", lambda e: e.activation(out=sq[0:bs, 0:nh, :], in_=x3, func=AF.Square), reads=[X], writes=[sq])
        yield
        P.op("dve", lambda e: e.tensor_reduce(out=ssq[0:bs, 0:nh], in_=sq[0:bs, 0:nh, :], axis=AX.X, op=ALU.add), reads=[sq], writes=[ssq])
        P.op("act", lambda e: e.activation(out=ssq[0:bs, 0:nh], in_=ssq[0:bs, 0:nh], func=AF.Ln, scale=1.0 / 128, bias=EPS), reads=[ssq], writes=[ssq])
        P.op("act", lambda e: e.activation(out=ssq[0:bs, 0:nh], in_=ssq[0:bs, 0:nh], func=AF.Exp, scale=-0.5), reads=[ssq], writes=[ssq])
        yield
        P.op("dve", lambda e: e.tensor_tensor(out=x3, in0=x3, in1=ssq[0:bs, 0:nh].unsqueeze(2).broadcast_to([bs, nh, 128]), op=ALU.mult), reads=[X, ssq], writes=[X])
        yield
        x4 = x3.rearrange("p (g h) d -> p g h d", g=ng)
        P.op("dve", lambda e: e.tensor_tensor(out=x4, in0=x4, in1=gain4, op=ALU.mult), reads=[X, gbuf], writes=[X])
        yield
        x1 = x3[:, :, 0:16]
        x2 = x3[:, :, 16:32]
        cb = cs.unsqueeze(1).broadcast_to([bs, nh, 16])
        sbb = sn.unsqueeze(1).broadcast_to([bs, nh, 16])
        t = [tmp[0:bs, i, 0:nh, :] for i in range(4)]
        P.op("dve", lambda e: e.tensor_tensor(out=t[0], in0=x1, in1=cb, op=ALU.mult), reads=[X, cbuf], writes=[tmp])
        P.op("dve", lambda e: e.tensor_tensor(out=t[1], in0=x2, in1=sbb, op=ALU.mult), reads=[X, cbuf], writes=[tmp])
        P.op("dve", lambda e: e.tensor_tensor(out=t[2], in0=x2, in1=cb, op=ALU.mult), reads=[X, cbuf], writes=[tmp])
        P.op("dve", lambda e: e.tensor_tensor(out=t[3], in0=x1, in1=sbb, op=ALU.mult), reads=[X, cbuf], writes=[tmp])
        yield
        P.op("dve", lambda e: e.tensor_tensor(out=x1, in0=t[0], in1=t[1], op=ALU.subtract), reads=[tmp], writes=[X])
        P.op("dve", lambda e: e.tensor_tensor(out=x2, in0=t[2], in1=t[3], op=ALU.add), reads=[tmp], writes=[X])
        yield

    def headnorm_rot_bf(st_bufs, x3, nh, bs, gain4b, cs, sn, ng, gbuf, cbuf, y3, Y):
        sq, ssq, tmp, X = st_bufs
        P.op("act", lambda e: e.activation(out=sq[0:bs, 0:nh, :], in_=x3, func=AF.Square), reads=[X], writes=[sq])
        yield
        P.op("dve", lambda e: e.tensor_reduce(out=ssq[0:bs, 0:nh], in_=sq[0:bs, 0:nh, :], axis=AX.X, op=ALU.add), reads=[sq], writes=[ssq])
        P.op("act", lambda e: e.activation(out=ssq[0:bs, 0:nh], in_=ssq[0:bs, 0:nh], func=AF.Ln, scale=1.0 / 128, bias=EPS), reads=[ssq], writes=[ssq])
        P.op("act", lambda e: e.activation(out=ssq[0:bs, 0:nh], in_=ssq[0:bs, 0:nh], func=AF.Exp, scale=-0.5), reads=[ssq], writes=[ssq])
        yield
        P.op("dve", lambda e: e.tensor_tensor(out=y3, in0=x3, in1=ssq[0:bs, 0:nh].unsqueeze(2).broadcast_to([bs, nh, 128]), op=ALU.mult), reads=[X, ssq], writes=[Y])
        yield
        y4 = y3.rearrange("p (g h) d -> p g h d", g=ng)
        P.op("dve", lambda e: e.tensor_tensor(out=y4, in0=y4, in1=gain4b, op=ALU.mult), reads=[Y, gbuf], writes=[Y])
        yield
        x1 = y3[:, :, 0:16]
        x2 = y3[:, :, 16:32]
        cb = cs.unsqueeze(1).broadcast_to([bs, nh, 16])
        sbb = sn.unsqueeze(1).broadcast_to([bs, nh, 16])
        t = [tmp[0:bs, i, 0:nh, :] for i in range(4)]
        P.op("dve", lambda e: e.tensor_tensor(out=t[0], in0=x1, in1=cb, op=ALU.mult), reads=[Y, cbuf], writes=[tmp])
        P.op("dve", lambda e: e.tensor_tensor(out=t[1], in0=x2, in1=sbb, op=ALU.mult), reads=[Y, cbuf], writes=[tmp])
        P.op("dve", lambda e: e.tensor_tensor(out=t[2], in0=x2, in1=cb, op=ALU.mult), reads=[Y, cbuf], writes=[tmp])
        P.op("dve", lambda e: e.tensor_tensor(out=t[3], in0=x1, in1=sbb, op=ALU.mult), reads=[Y, cbuf], writes=[tmp])
        yield
        P.op("dve", lambda e: e.tensor_tensor(out=x1, in0=t[0], in1=t[1], op=ALU.subtract), reads=[tmp], writes=[Y])
        P.op("dve", lambda e: e.tensor_tensor(out=x2, in0=t[2], in1=t[3], op=ALU.add), reads=[tmp], writes=[Y])
        yield

    def kv_phase():
        TT = 256
        with contextlib.ExitStack() as st:
            w = sb("w_kvs", [128, 8, 1536], BF16, st)
            load_w(w, w_kv, 8, 1536, 4)
            N = NormCtx(st, TT, kv_norm, nxt=3)
            kg = sb("kg", [128, 3, 128], F32, st)
            P.dma("sp", kg[:], k_norm.partition_broadcast(128), writes=[kg])
            csL = [sb("cs%d" % i, [128, 2, 2, 16], F32, st) for i in range(3)]
            kcL = [sb("kc%d" % i, [128, 6, 128], F32, st) for i in range(2)]
            vvL = [sb("vv%d" % i, [128, 3, 256], F32, st) for i in range(2)]
            k16L = [sb("kvh16%d" % i, [128, 3, 512], BF16, st) for i in range(2)]
            sqL = [sb("hsq%d" % i, [128, 6, 128], F32, st) for i in range(2)]
            ssqL = [sb("hssq%d" % i, [128, 6], F32, st) for i in range(2)]
            tmpL = [sb("htmp%d" % i, [128, 4, 6, 16], F32, st) for i in range(2)]
            tctx = {}

            TL = tile_list(TT)

            lctx = {}

            def preload(ti):
                smp, s0, n = TL[ti]
                bs = min(128, n)
                nb = n // bs
                src = (xss if smp else xs[s0:s0 + n, :])
                xt = N.load(src, n, xbufs(smp, s0, n))
                cs = csL[ti % 3]
                cc, ss_ = (c_cos_s, c_sin_s) if smp else (c_cos[s0:s0 + n, :], c_sin[s0:s0 + n, :])
                P.dma("act", cs[0:bs, 0:nb, 0, :], cc.rearrange("(b p) f -> p b f", p=bs), writes=[cs])
                P.dma("act", cs[0:bs, 0:nb, 1, :], ss_.rearrange("(b p) f -> p b f", p=bs), writes=[cs])
                lctx[ti] = (xt, cs)

            def prep(ti):
                smp, s0, n = TL[ti]
                if ti not in lctx:
                    preload(ti)
                xt, cs = lctx[ti]
                tctx[ti] = (N.norm(xt, n), cs)

            def blk_gen(ti, smp, s0, n, b):
                bs = min(128, n)
                nb = n // bs
                if ti == 0 and b == 0:
                    prep(0)
                if b == 0 and ti + 1 < len(TL):
                    preload(ti + 1)
                if b == nb - 1 and ti + 1 < len(TL):
                    prep(ti + 1)
                    yield
                xnT, cs = tctx[ti]
                kc, vv, kvh16 = kcL[b & 1], vvL[b & 1], k16L[b & 1]
                cols = slice(b * bs, (b + 1) * bs)
                for g in range(3):
                    ps = PSF()
                    for kc_ in range(8):
                        mm(ps[0:bs, :], xnT[:, kc_, cols], w[:, kc_, g * 512:(g + 1) * 512], kc_ == 0, kc_ == 7, [w, xnT], [ps])
                    P.op("act", lambda e: e.copy(out=kc[0:bs, 2 * g:2 * g + 2, :], in_=ps[0:bs, 0:256].rearrange("p (h d) -> p h d", h=2)), reads=[ps], writes=[kc])
                    P.op("act", lambda e: e.copy(out=vv[0:bs, g, :], in_=ps[0:bs, 256:512]), reads=[ps], writes=[vv])
                    yield
                for _ in headnorm_rot((sqL[b & 1], ssqL[b & 1], tmpL[b & 1], kc), kc[0:bs, :, :], 6, bs,
                                      kg[0:bs, :, :].unsqueeze(2).broadcast_to([bs, 3, 2, 128]), cs[0:bs, b, 0, :], cs[0:bs, b, 1, :], 3, kg, cs):
                    yield
                if smp:
                    dk = kvs.rearrange("p (g c) -> p g c", g=3)
                    P.dma("sp", dk[:, :, 0:256], kc[0:bs, :, :].rearrange("p (g h) d -> p g (h d)", g=3), reads=[kc], writes=[B_kvs], sb=kc)
                    P.dma("sp", dk[:, :, 256:512], vv[0:bs, :, :], reads=[vv], writes=[B_kvs], sb=vv)
                else:
                    r0 = s0 + b * bs
                    dk = kvp[r0:r0 + bs, :].rearrange("p (g c) -> p g c", g=3)
                    dkb = kvb16[r0:r0 + bs, :].rearrange("p (g c) -> p g c", g=3)
                    P.dma("sp", dk[:, :, 0:256], kc[0:bs, :, :].rearrange("p (g h) d -> p g (h d)", g=3), reads=[kc], writes=[B_kvp], sb=kc)
                    P.dma("sp", dk[:, :, 256:512], vv[0:bs, :, :], reads=[vv], writes=[B_kvp], sb=vv)
                    P.op("act", lambda e: e.copy(out=kvh16[0:bs, :, 0:256], in_=kc[0:bs, :, :].rearrange("p (g h) d -> p g (h d)", g=3)), reads=[kc], writes=[kvh16])
                    P.op("pool", lambda e: e.tensor_copy(out=kvh16[0:bs, :, 256:512], in_=vv[0:bs, :, :]), reads=[vv], writes=[kvh16])
                    P.dma("sp", dkb[:, :, :], kvh16[0:bs, :, :], reads=[kvh16], writes=[B_kvb], sb=kvh16)

            gens = []
            for ti, (smp, s0, n) in enumerate(tile_list(TT)):
                for b in range(max(1, n // 128)):
                    gens.append(blk_gen(ti, smp, s0, n, b))
            pipeline(gens, depth=2, lag=7)
        end_phase()

    def kv_exchange():
        for i, (g_, o_, n_) in enumerate(WX):
            r0 = T - WINS[g_] + o_
            P.dma("pool", wx_t[i][:, :], kvp[r0:r0 + n_, g_ * 512:(g_ + 1) * 512], reads=[B_kvp], writes=[B_wx[i]], sb=B_wx[i])
        for i, (g_, o_, n_) in enumerate(WX):
            P.collective(wx_t[i].ap().opt(), wg_t[i].ap().opt(), GROUPS, [B_wx[i]], [B_kvg])
        for i, (g_, o_, n_) in enumerate(WX):
            P.dma("pool", wgb[g_][o_:o_ + n_, :], wg_t[i][0:n_, :], reads=[B_kvg], writes=[B_wgb], sb=B_wgb, chain=False)
        for g in range(3):
            P.dma("pool", winp[g], kvp[T - WINS[g]:T, g * 512:(g + 1) * 512], reads=[B_kvp], writes=[B_out], sb=B_out, chain=False)
        for g in range(3):
            flat = cache[g].rearrange("s t f -> (s t) f")
            rows = NSQ * WINS[g]
            step = min(rows, 1024)
            for r0 in range(0, rows, step):
                P.dma("pool", cb16[g][r0:r0 + step, :], flat[r0:r0 + step, :], writes=[B_cb], sb=B_cb, chain=False)
        P.dma("pool", kvs16, kvs, reads=[B_kvs], writes=[B_cb], sb=B_cb, chain=False)

    def q_phase(j):
        TT = 256
        with contextlib.ExitStack() as st:
            wq = [sb("w_q%d" % i, [128, 8, 512], BF16, st) for i in range(6)]
            vq = b_w_q[j].rearrange("(c p) f -> p c f", p=128)
            for i in range(6):
                P.dma("pool", wq[i][:], vq[:, :, i * 512:(i + 1) * 512], writes=[wq[i]])
            if j == 0:
                kv_exchange()
            N = NormCtx(st, TT, b_norm[j], nxt=3)
            qg = sb("qg", [128, 3, 128], F32, st)
            P.dma("sp", qg[:], q_norm[j].partition_broadcast(128), writes=[qg])
            qg16 = sb("qg16", [128, 3, 128], BF16, st)
            P.op("dve", lambda e: e.tensor_copy(out=qg16[:], in_=qg[:]), reads=[qg], writes=[qg16])
            csL = [sb("cs%d" % i, [128, 2, 2, 16], F32, st) for i in range(3)]
            qt = [sb("qt%d" % i, [128, 24, 128], F32, st) for i in range(2)]
            qb = [sb("qb%d" % i, [128, 3072], BF16, st) for i in range(2)]
            sqL = [sb("hsq%d" % i, [128, 24, 128], BF16, st) for i in range(2)]
            ssqL = [sb("hssq%d" % i, [128, 24], F32, st) for i in range(2)]
            tmpL = [sb("htmp%d" % i, [128, 4, 24, 16], F32, st) for i in range(2)]
            tctx = {}

            TL = tile_list(TT)

            lctx = {}

            def preload(ti):
                smp, s0, n = TL[ti]
                bs = min(128, n)
                nb = n // bs
                src = (xss if smp else xs[s0:s0 + n, :])
                xt = N.load(src, n, xbufs(smp, s0, n))
                cs = csL[ti % 3]
                cc, ss_ = (c_cos_s, c_sin_s) if smp else (c_cos[s0:s0 + n, :], c_sin[s0:s0 + n, :])
                P.dma("act", cs[0:bs, 0:nb, 0, :], cc.rearrange("(b p) f -> p b f", p=bs), writes=[cs])
                P.dma("act", cs[0:bs, 0:nb, 1, :], ss_.rearrange("(b p) f -> p b f", p=bs), writes=[cs])
                lctx[ti] = (xt, cs)

            def prep(ti):
                smp, s0, n = TL[ti]
                if ti not in lctx:
                    preload(ti)
                xt, cs = lctx[ti]
                tctx[ti] = (N.norm(xt, n), cs)

            def blk_gen(ti, smp, s0, n, b, gi):
                bs = min(128, n)
                nb = n // bs
                if ti == 0 and b == 0:
                    prep(0)
                if b == 0 and ti + 1 < len(TL):
                    preload(ti + 1)
                if b == nb - 1 and ti + 1 < len(TL):
                    prep(ti + 1)
                    yield
                xnT, cs = tctx[ti]
                cols = slice(b * bs, (b + 1) * bs)
                q_ = qt[gi & 1]
                qb_ = qb[gi & 1]
                for c6 in range(6):
                    ps = PSF()
                    for kc_ in range(8):
                        mm(ps[0:bs, :], xnT[:, kc_, cols], wq[c6][:, kc_, :], kc_ == 0, kc_ == 7, [wq[c6], xnT], [ps])
                    P.op("act", lambda e: e.copy(out=q_[0:bs, c6 * 4:(c6 + 1) * 4, :], in_=ps[0:bs, :].rearrange("p (h d) -> p h d", h=4)), reads=[ps], writes=[q_])
                    yield
                for _ in headnorm_rot_bf((sqL[gi & 1], ssqL[gi & 1], tmpL[gi & 1], q_), q_[0:bs, :, :], 24, bs,
                                         qg16[0:bs, :, :].unsqueeze(2).broadcast_to([bs, 3, 8, 128]), cs[0:bs, b, 0, :], cs[0:bs, b, 1, :], 3, qg16, cs,
                                         qb_[0:bs, :].rearrange("p (h d) -> p h d", h=24), qb_):
                    yield
                if smp:
                    P.dma("sp", qs, qb_[0:bs, :], reads=[qb_], writes=[B_qs], sb=qb_)
                else:
                    P.dma("sp", qp[s0 + b * bs:s0 + (b + 1) * bs, :], qb_[0:bs, :], reads=[qb_], writes=[B_qp], sb=qb_)

            gens = []
            for ti, (smp, s0, n) in enumerate(tile_list(TT)):
                for b in range(max(1, n // 128)):
                    gens.append(blk_gen(ti, smp, s0, n, b, len(gens)))
            pipeline(gens, depth=2, lag=7)
        end_phase()

    def att_phase(j, dst_p, dst_s, last):
        with contextlib.ExitStack() as st:
            w_o = sb("w_o", [128, 8, D], BF16, st)
            load_w(w_o, b_w_o[j], 8, D, 2)
            numT = sb("numT", [128, 4, 2048], F32, st)
            denT = sb("denT", [128, 4, 2048], F32, st)
            oT = sb("oT", [128, 8, 2048], BF16, st)
            qb = [sb("aqb%d" % i, [128, 512], BF16, st) for i in range(8)]
            QTb = [sb("aQT%d" % i, [128, 512], BF16, st) for i in range(8)]
            kvb = [sb("akv%d" % i, [128, 2, 128], BF16, st) for i in range(16)]
            KTb = [sb("aKT%d" % i, [128, 128], BF16, st) for i in range(16)]
            Eb = [sb("aE%d" % i, [128, 512], BF16, st) for i in range(8)]
            Pb = [sb("aP%d" % i, [128, 512], BF16, st) for i in range(8)]
            xt = [sb("axt%d" % i, [128, D], F32, st) for i in range(2)]
            xo = [sb("axo%d" % i, [128, D], F32, st) for i in range(2)]
            cnt = {"q": 0, "k": 0, "e": 0, "x": 0}
            mbias = sb("mbias", [128, 3, 512], BF16, st)
            P.dma("pool", mbias[:], c_mbias, writes=[mbias])

            def load_kv_dma(g, kvh, tok0, dil, src_rows=None):
                cnt["k"] += 1
                kb, kt_ = kvb[cnt["k"] % 16], KTb[cnt["k"] % 16]
                if src_rows is None:
                    if tok0 >= 0:
                        b5 = kvb16.rearrange("t (g c h d) -> t g c h d", g=3, c=2, h=2)
                        P.dma("sp", kb[:], b5[ss(tok0, 128, dil), g, :, kvh, :], reads=[B_kvb], writes=[kb])
                    else:
                        r_ = tok0 + 128 * dil
                        w4 = wgb[g].rearrange("t (c h d) -> t c h d", c=2, h=2)
                        P.dma("sp", kb[:], w4[ss(r_, 128, dil), :, kvh, :], reads=[B_wgb], writes=[kb])
                else:
                    for i_, (p0, np_, ap_) in enumerate(src_rows):
                        P.dma("sp", kb[p0:p0 + np_, :, :], ap_, reads=[B_cb], writes=[kb], chain=(i_ == 0))
                return kb, kt_

            def kv_transpose(kb, kt_):
                pb = PSB()
                tr(pb[:, 0:128], kb[:, 0, :], 128, [kb], [pb])
                P.op("act", lambda e: e.copy(out=kt_[:], in_=pb[:, 0:128]), reads=[pb], writes=[kt_])

            def load_kv(g, kvh, tok0, dil, src_rows=None):
                kb, kt_ = load_kv_dma(g, kvh, tok0, dil, src_rows)
                kv_transpose(kb, kt_)
                return kb, kt_

            own_of = {}

            def blk_gen(kvh, g, r, bl):
                dil = DILS[g]
                tok0 = dil * 128 * bl + r
                cnt["q"] += 1
                q_, QT_ = qb[cnt["q"] % 8], QTb[cnt["q"] % 8]
                c0 = g * 1024 + kvh * 512
                P.dma("sp", q_[:], qp[ss(tok0, 128, dil), c0:c0 + 512], reads=[B_qp], writes=[q_])
                pv_new = load_kv_dma(g, kvh, tok0 - 128 * dil, dil) if bl == 0 else None
                own = load_kv_dma(g, kvh, tok0, dil)
                own_of[(kvh, g, r, bl)] = own
                for _ in range(3):
                    yield
                pb = PSB()
                for hh in range(4):
                    tr(pb[:, hh * 128:(hh + 1) * 128], q_[:, hh * 128:(hh + 1) * 128], 128, [q_], [pb])
                P.op("act", lambda e: e.copy(out=QT_[:], in_=pb[:, 0:512]), reads=[pb], writes=[QT_])
                if pv_new is not None:
                    kv_transpose(*pv_new)
                    prev = pv_new
                else:
                    prev = own_of[(kvh, g, r, bl - 1)]
                kv_transpose(*own)
                yield
                parts = [(own, 2), (prev, 4 if bl == 0 else 1)]
                Ps = []
                for (kb, kt_), mi in parts:
                    cnt["e"] += 1
                    E_, P_ = Eb[cnt["e"] % 8], Pb[cnt["e"] % 8]
                    ps = PSF()
                    mm(ps[:, :], kt_[:, :], QT_[:, :], True, False, [kt_, QT_], [ps])
                    mm(ps[:, :], ident[:, :], mbias[:, {2: 0, 1: 1, 4: 2}[mi], :], False, True, [ident, mbias], [ps])
                    P.op("act", lambda e: e.activation(out=E_[:], in_=ps[:, :], func=AF.Exp, scale=SCALE), reads=[ps], writes=[E_])
                    Ps.append((kb, E_))
                    yield
                po = PSF()
                for i, (kb, P_) in enumerate(Ps):
                    mm(po[:, :], kb[:, 1, :], P_[:, :], i == 0, i == len(Ps) - 1, [kb, P_], [po])
                pd = PSF()
                for i, (kb, P_) in enumerate(Ps):
                    mm(pd[:, :], ones[:, :], P_[:, :], i == 0, i == len(Ps) - 1, [ones, P_], [pd])
                nv = numT[:, :, ss(tok0, 128, dil)]
                dv = denT[:, :, ss(tok0, 128, dil)]
                po3 = po[:, :].rearrange("p (h t) -> p h t", h=4)
                pd3 = pd[:, :].rearrange("p (h t) -> p h t", h=4)
                if g == 0:
                    P.op("act", lambda e: e.copy(out=nv, in_=po3), reads=[po], writes=[numT])
                    P.op("dve", lambda e: e.tensor_copy(out=dv, in_=pd3), reads=[pd], writes=[denT])
                else:
                    P.op("dve", lambda e: e.tensor_tensor(out=nv, in0=po3, in1=nv, op=ALU.add), reads=[po, numT], writes=[numT])
                    P.op("dve", lambda e: e.tensor_tensor(out=dv, in0=pd3, in1=dv, op=ALU.add), reads=[pd, denT], writes=[denT])

            for kvh in range(2):
                gens = [blk_gen(kvh, g, r, bl) for g in range(3) for r in range(DILS[g]) for bl in range(16 // DILS[g])]
                pipeline(gens, depth=7, lag=1)
                for hh in range(4):
                    P.op("act", lambda e: e.activation(out=denT[:, hh, :], in_=denT[:, hh, :], func=AF.Ln), reads=[denT], writes=[denT])
                    P.op("act", lambda e: e.activation(out=denT[:, hh, :], in_=denT[:, hh, :], func=AF.Exp, scale=-1.0), reads=[denT], writes=[denT])
                    P.op("pool", lambda e: e.tensor_tensor(out=oT[:, kvh * 4 + hh, :], in0=numT[:, hh, :], in1=denT[:, hh, :], op=ALU.mult), reads=[numT, denT], writes=[oT])

            def wo_gen(tb):
                cnt["x"] += 1
                x_, o_ = xt[cnt["x"] & 1], xo[cnt["x"] & 1]
                P.dma("sp", x_[:], xs[tb * 128:(tb + 1) * 128, :], reads=[B_xs[tb]], writes=[x_])
                yield
                for hf in range(2):
                    ps = PSF()
                    for h in range(8):
                        mm(ps[:, :], oT[:, h, tb * 128:(tb + 1) * 128], w_o[:, h, hf * 512:(hf + 1) * 512], h == 0, h == 7, [oT, w_o], [ps])
                    P.op("dve", lambda e: e.tensor_tensor(out=o_[:, hf * 512:(hf + 1) * 512], in0=ps[:, :], in1=x_[:, hf * 512:(hf + 1) * 512], op=ALU.add), reads=[ps, x_], writes=[o_])
                    yield
                P.dma("sp", dst_p[tb * 128:(tb + 1) * 128, :], o_[:], reads=[o_], writes=[B_xs[tb]], sb=o_)

            pipeline([wo_gen(tb) for tb in range(16)], depth=2, lag=1)

            qsb = sb("qsb", [16, 3072], BF16, st)
            QTs = sb("QTs", [128, 24, 16], BF16, st)
            ksf = sb("ksf", [16, 3, 2, 2, 128], BF16, st)
            KsT = sb("KsT", [128, 6, 16], BF16, st)
            numS = sb("numS", [128, 2, 4, 16], F32, st)
            denS = sb("denS", [128, 2, 4, 16], F32, st)
            oTs = sb("oTs", [128, 8, 16], BF16, st)
            Es = [sb("Es%d" % i, [128, 64], BF16, st) for i in range(2)]
            P.dma("sp", qsb[:], qs, reads=[B_qs], writes=[qsb])
            P.dma("pool", ksf[:], kvs.rearrange("t (g c h d) -> t g c h d", g=3, c=2, h=2), reads=[B_kvs], writes=[ksf])
            pb = PSB()
            for gh in range(24):
                tr(pb[:, gh * 16:(gh + 1) * 16], qsb[0:16, gh * 128:(gh + 1) * 128], 16, [qsb], [pb])
            P.op("act", lambda e: e.copy(out=QTs[:, :, :], in_=pb[:, 0:384].rearrange("p (a t) -> p a t", a=24)), reads=[pb], writes=[QTs])
            pb = PSB()
            for g in range(3):
                for kvh in range(2):
                    tr(pb[:, (g * 2 + kvh) * 16:(g * 2 + kvh + 1) * 16], ksf[0:16, g, 0, kvh, :], 16, [ksf], [pb])
            P.op("act", lambda e: e.copy(out=KsT[:, :, :], in_=pb[:, 0:96].rearrange("p (a t) -> p a t", a=6)), reads=[pb], writes=[KsT])
            ec = 0
            for kvh in range(2):
                for g in range(3):
                    dil = DILS[g]
                    ps = PSF()
                    for hh in range(4):
                        mm(ps[0:16, hh * 16:(hh + 1) * 16], KsT[:, g * 2 + kvh, :], QTs[:, g * 8 + kvh * 4 + hh, :], True, True, [KsT, QTs], [ps])
                    ec += 1
                    E_ = Es[ec & 1]
                    P.op("act", lambda e: e.activation(out=E_[0:16, 0:64], in_=ps[0:16, 0:64], func=AF.Exp, scale=SCALE), reads=[ps], writes=[E_])
                    P.op("dve", lambda e: e.tensor_tensor(out=E_[0:16, 0:64].rearrange("p (h t) -> p h t", h=4), in0=E_[0:16, 0:64].rearrange("p (h t) -> p h t", h=4),
                                                          in1=ident[0:16, 0:16].unsqueeze(1).broadcast_to([16, 4, 16]), op=ALU.mult), reads=[E_, ident], writes=[E_])
                    po = PSF()
                    mm(po[:, 0:64], ksf[0:16, g, 1, kvh, :], E_[0:16, 0:64], True, True, [ksf, E_], [po])
                    pd = PSF()
                    mm(pd[:, 0:64], ones[0:16, :], E_[0:16, 0:64], True, True, [ones, E_], [pd])
                    nvs = numS[:, kvh, :, :].rearrange("p h t -> p (h t)")
                    dvs = denS[:, kvh, :, :].rearrange("p h t -> p (h t)")
                    if g == 0:
                        P.op("act", lambda e: e.copy(out=nvs, in_=po[:, 0:64]), reads=[po], writes=[numS])
                        P.op("dve", lambda e: e.tensor_copy(out=dvs, in_=pd[:, 0:64]), reads=[pd], writes=[denS])
                    else:
                        P.op("dve", lambda e: e.tensor_tensor(out=nvs, in0=po[:, 0:64], in1=nvs, op=ALU.add), reads=[po, numS], writes=[numS])
                        P.op("dve", lambda e: e.tensor_tensor(out=dvs, in0=pd[:, 0:64], in1=dvs, op=ALU.add), reads=[pd, denS], writes=[denS])
                    c5 = cb16[g].rearrange("(s t) (c h d) -> s t c h d", s=NSQ, c=2, h=2)
                    k5 = kvs16.rearrange("t (g c h d) -> t g c h d", g=3, c=2, h=2)

                    def unit_gen(kvh, g, dil, sq_, t, c5, k5):
                        tok = sq_ * 4 + t
                        if g == 0 and t > 0:
                            rows = [(0, 128 - t, c5[sq_, t:128, :, kvh, :]), (128 - t, t, k5[sq_ * 4:sq_ * 4 + t, g, :, kvh, :])]
                        else:
                            rows = [(0, 128, c5[sq_, ss(t, 128, dil), :, kvh, :])]
                        kb, kt_ = load_kv_dma(g, kvh, 0, dil, src_rows=rows)
                        for _ in range(4):
                            yield
                        kv_transpose(kb, kt_)
                        yield
                        ps = PSF()
                        q4 = QTs[:, g * 8 + kvh * 4:g * 8 + kvh * 4 + 4, tok]
                        mm(ps[:, 0:4], kt_[:, :], q4, True, True, [kt_, QTs], [ps])
                        cnt["e"] += 1
                        E_ = Eb[cnt["e"] % 8]
                        P.op("act", lambda e: e.activation(out=E_[:, 0:4], in_=ps[:, 0:4], func=AF.Exp, scale=SCALE), reads=[ps], writes=[E_])
                        yield
                        po = PSF()
                        mm(po[:, 0:4], kb[:, 1, :], E_[:, 0:4], True, True, [kb, E_], [po])
                        pd = PSF()
                        mm(pd[:, 0:4], ones[:, :], E_[:, 0:4], True, True, [ones, E_], [pd])
                        P.op("dve", lambda e: e.tensor_tensor(out=numS[:, kvh, :, tok], in0=po[:, 0:4], in1=numS[:, kvh, :, tok], op=ALU.add), reads=[po, numS], writes=[numS])
                        P.op("dve", lambda e: e.tensor_tensor(out=denS[:, kvh, :, tok], in0=pd[:, 0:4], in1=denS[:, kvh, :, tok], op=ALU.add), reads=[pd, denS], writes=[denS])

                    pipeline([unit_gen(kvh, g, dil, sq_, t, c5, k5) for sq_ in range(NSQ) for t in range(4)], depth=8, lag=1)
            P.op("act", lambda e: e.activation(out=denS[:], in_=denS[:], func=AF.Ln), reads=[denS], writes=[denS])
            P.op("act", lambda e: e.activation(out=denS[:], in_=denS[:], func=AF.Exp, scale=-1.0), reads=[denS], writes=[denS])
            P.op("dve", lambda e: e.tensor_tensor(out=oTs[:, :, :], in0=numS[:, :, :, :].rearrange("p k h t -> p (k h) t"), in1=denS[:, :, :, :].rearrange("p k h t -> p (k h) t"), op=ALU.mult),
                 reads=[numS, denS], writes=[oTs])
            x_, o_ = xt[0], xo[0]
            P.dma("sp", x_[0:16, :], xss, reads=[B_xss], writes=[x_])
            for hf in range(2):
                ps = PSF()
                for h in range(8):
                    mm(ps[0:16, :], oTs[:, h, :], w_o[:, h, hf * 512:(hf + 1) * 512], h == 0, h == 7, [oTs, w_o], [ps])
                P.op("dve", lambda e: e.tensor_tensor(out=o_[0:16, hf * 512:(hf + 1) * 512], in0=ps[0:16, :], in1=x_[0:16, hf * 512:(hf + 1) * 512], op=ALU.add), reads=[ps, x_], writes=[o_])
            P.dma("sp", dst_s, o_[0:16, :], reads=[o_], writes=[B_xss], sb=o_)
        end_phase()

    def full():
        hgrn_layer(0, xp, xsm, xs, xss, True)
        mlp_phase(0, xs, xss, xs, xss, False, False)
        hgrn_layer(1, xs, xss, xs, xss, False)
        mlp_phase(1, xs, xss, xs, xss, False, False)
        kv_phase()
        q_phase(0)
        att_phase(0, xs, xss, False)
        mlp_phase(2, xs, xss, xs, xss, False, False)
        q_phase(1)
        att_phase(1, xs, xss, False)
        mlp_phase(3, xs, xss, yp, ysm, False, True)
        P.barrier()

    if mode == "full":
        full()
        return nc, P
    if mode == "att":
        for i in range(T // 128):
            P.dma("sp", xs[i * 128:(i + 1) * 128, :], xp[i * 128:(i + 1) * 128, :], writes=[B_xs[i]], sb=B_xs[i])
        P.dma("sp", xss, xsm, writes=[B_xss], sb=B_xss)
        P.barrier()
        kv_phase()
        q_phase(0)
        att_phase(0, yp, ysm, False)
        P.barrier()
        return nc, P
    if mode == "hgrn":
        hgrn_phase(0, xp, xsm, yp, ysm, True)
        P.barrier()
        return nc, P
    if mode == "mlp":
        mlp_phase(0, xp, xsm, yp, ysm, True, True)
        P.barrier()
        return nc, P
    return nc, P


def _consts(hp=0):
    half = 16
    inv = (np.float32(THETA) ** (-2.0 * np.arange(half, dtype=np.float32) / np.float32(32))).astype(np.float32)
    pos = np.arange(T, dtype=np.float32) + np.float32(hp * T)
    ang = (pos[:, None] * inv[None, :]).astype(np.float32)
    pos_s = np.tile(np.arange(4, dtype=np.float32) + np.float32(PAST), NSQ)
    ang_s = (pos_s[:, None] * inv[None, :]).astype(np.float32)
    u = np.arange(128)[:, None]
    t = np.arange(128)[None, :]
    m = np.zeros((128, 5, 128), np.float32)
    m[:, 0] = ((u // 64) == (t // 64)) & (t >= u)
    m[:, 1] = (u >= t)
    m[:, 2] = (u <= t)
    m[:, 3] = ((u // 4) == (t // 4)) & (t >= u)
    m[:, 4] = m[:, 1] * float(hp)
    sc = np.ones((2, 2048), np.float32)
    sc[0, ::64] = 0
    sc[1, ::4] = 0
    return {
        "c_cos": np.cos(ang).astype(np.float32), "c_sin": np.sin(ang).astype(np.float32),
        "c_cos_s": np.cos(ang_s).astype(np.float32), "c_sin_s": np.sin(ang_s).astype(np.float32),
        "c_ident": np.eye(128, dtype=np.float32), "c_masks": m, "c_scan": sc,
        "c_rowm": (np.arange(128)[:, None] // 4 == np.arange(4)[None, :]).astype(np.float32),
        "c_flag": np.full((128, 1), float(hp), np.float32),
        "c_mbias": np.stack([np.tile(np.where(m[:, k] > 0, 0.0, -30000.0), (1, 4)) for k in (2, 1, 4)], axis=1).astype(np.float32),
    }


_CACHE = {}


def kernel(**inputs):
    inp = {k: np.ascontiguousarray(np.asarray(v)) for k, v in inputs.items()}
    if "nc" not in _CACHE:
        _CACHE["nc"] = build("full")[0]
    nc = _CACHE["nc"]
    consts = [_consts(0), _consts(1)]
    wnames = ["a_norm", "a_w_in", "a_lb_logits", "a_out_norm", "a_w_out", "kv_norm", "w_kv", "k_norm",
              "b_norm", "b_w_q", "q_norm", "b_w_o", "mlp_norm", "mlp_w_up", "mlp_w_down"]
    in_maps = []
    for c in range(8):
        m = {k: inp[k] for k in wnames}
        m.update(consts[c % 2])
        m["xp"] = inp["x_prompt"][c // 2, (c % 2) * T:(c % 2 + 1) * T]
        m["xsm"] = inp["x_sample"][4 * c:4 * c + 4].reshape(NST, D)
        m["st_in"] = np.ascontiguousarray(inp["state_hgrn"][:, 4 * c:4 * c + 4])
        m["c1"] = inp["cache_win1_kv"][4 * c:4 * c + 4].reshape(NSQ, 128, 512)
        m["c2"] = inp["cache_win2_kv"][4 * c:4 * c + 4].reshape(NSQ, 512, 512)
        m["c3"] = inp["cache_win3_kv"][4 * c:4 * c + 4].reshape(NSQ, 2048, 512)
        in_maps.append(m)
    res = run_bass_kernel_spmd(nc, in_maps, core_ids=list(range(8))).results
    y_prompt = np.stack([np.concatenate([res[2 * i]["yp"], res[2 * i + 1]["yp"]], axis=0) for i in range(4)], axis=0)
    y_sample = np.concatenate([res[c]["ysm"].reshape(NSQ, 4, D) for c in range(8)], axis=0)
    st_p = np.stack([res[2 * i + 1]["stp"] for i in range(4)], axis=1)
    st_s = np.concatenate([res[c]["sts"] for c in range(8)], axis=1)
    wins_p = [np.stack([res[c]["w%dp" % (g + 1)].reshape(WINS[g], 2, 2, 128) for c in (1, 3, 5, 7)], axis=0) for g in range(3)]
    wins_s = [np.concatenate([res[c]["kvs"][:, g * 512:(g + 1) * 512].reshape(NSQ, 4, 2, 2, 128) for c in range(8)], axis=0) for g in range(3)]
    f = lambda a: np.ascontiguousarray(a, dtype=np.float32)
    return (f(y_prompt), f(y_sample), f(st_p), f(st_s), f(wins_p[0]), f(wins_p[1]), f(wins_p[2]),
            f(wins_s[0]), f(wins_s[1]), f(wins_s[2]))
```
